# Optimizing a Trainium2 kernel written in Bass

```python
import jax, jax.numpy as jnp
from jax import lax
import numpy as np

D_MODEL = 1024
BATCH = 16
SEQ = 256
DEPTH = 2
DEC_BATCH = 4
DEC_SEQ = 2048
PAST_LEN = 256

GRID_W = 64
N_EVEN = (DEPTH + 1) // 2
N_ODD = DEPTH // 2
HEAD_DIM = 64
ATTN_WIDTH = D_MODEL // 2
N_Q_HEADS = ATTN_WIDTH // HEAD_DIM
N_KV_HEADS = N_Q_HEADS // 4
Q_PER_KV = N_Q_HEADS // N_KV_HEADS
KV_WIDTH = N_KV_HEADS * HEAD_DIM
CONV_WIDTH = D_MODEL - ATTN_WIDTH
WINDOW = 128
BLOCK = 128
ROPE_BASE = 10000.0
ROPE_FREQS = HEAD_DIM // 4
ATTN_SCALE = HEAD_DIM ** -0.5
NEG = -1e30
POOL_WIDTH = D_MODEL
POOL_SIZES = (2, 4, 8, 16)
N_POOL_GROUPS = len(POOL_SIZES)
POOL_GROUP = POOL_WIDTH // N_POOL_GROUPS
EPS = 1e-6
EVEN_SIZES = (CONV_WIDTH, CONV_WIDTH, CONV_WIDTH, CONV_WIDTH, ATTN_WIDTH, KV_WIDTH, KV_WIDTH, ATTN_WIDTH)
EVEN_IN = sum(EVEN_SIZES)
EVEN_SPLITS = tuple(int(s) for s in np.cumsum(EVEN_SIZES)[:-1])
ODD_IN = 2 * POOL_WIDTH

kernel_name = "hybrid_diffusion_prefix_conv_swa_pool_step"


def rmsnorm(x, g):
    xf = x.astype(jnp.float32)
    y = xf * lax.rsqrt(jnp.mean(xf * xf, axis=-1, keepdims=True) + EPS)
    return (y * g.astype(jnp.float32)).astype(x.dtype)


def adaln(cond, w, b):
    m = jax.nn.silu(cond) @ w + b
    shift, scale, gate = jnp.split(m, 3, axis=-1)
    return shift[:, None], scale[:, None], gate[:, None]


def modulate(x, g, shift, scale):
    return rmsnorm(x, g) * (1 + scale) + shift


def short_conv(u, w, b):
    up = jnp.pad(u, ((0, 0), (1, 1), (0, 0)))
    return up[:, :-2] * w[0] + up[:, 1:-1] * w[1] + up[:, 2:] * w[2] + b


def axial_rope_tables(n_rows):
    row = jnp.repeat(jnp.arange(n_rows), GRID_W).astype(jnp.float32)
    col = jnp.tile(jnp.arange(GRID_W), n_rows).astype(jnp.float32)
    inv = ROPE_BASE ** (-jnp.arange(ROPE_FREQS, dtype=jnp.float32) / ROPE_FREQS)
    ang = jnp.stack([row[:, None] * inv, col[:, None] * inv], axis=1)
    return jnp.cos(ang), jnp.sin(ang)


def apply_axial_rope(x, cos, sin):
    B, L, H, _ = x.shape
    xr = x.astype(jnp.float32).reshape(B, L, H, 2, 2, ROPE_FREQS)
    x1, x2 = xr[..., 0, :], xr[..., 1, :]
    c, s = cos[None, :, None], sin[None, :, None]
    out = jnp.stack([x1 * c - x2 * s, x2 * c + x1 * s], axis=-2)
    return out.reshape(x.shape).astype(x.dtype)


def sink_softmax(s, sink):
    m = jnp.maximum(jnp.max(s, axis=-1, keepdims=True), sink)
    e = jnp.exp(s - m)
    return e / (jnp.sum(e, axis=-1, keepdims=True) + jnp.exp(sink - m))


def context_attention(q, k, v, sink):
    B, S = q.shape[:2]
    nb = S // BLOCK
    qb = q.reshape(B, nb, BLOCK, N_KV_HEADS, Q_PER_KV, HEAD_DIM).swapaxes(0, 1).astype(jnp.float32)
    kf, vf = k.astype(jnp.float32), v.astype(jnp.float32)
    sk = sink.astype(jnp.float32)[None, :, :, None, None]

    def attend(qblk):
        s = jnp.einsum('bqkgd,bskd->bkgqs', qblk, kf) * ATTN_SCALE
        p = sink_softmax(s, sk)
        return jnp.einsum('bkgqs,bskd->bqkgd', p, vf)

    o = lax.map(attend, qb)
    return o.swapaxes(0, 1).reshape(B, S, ATTN_WIDTH).astype(q.dtype)


def latent_attention(q, k, v, ctx_k, ctx_v, sink):
    B, L = q.shape[:2]
    nb = L // BLOCK
    pad = ((0, 0), (BLOCK, BLOCK), (0, 0), (0, 0))
    kblk = jnp.pad(k, pad).reshape(B, nb + 2, BLOCK, N_KV_HEADS, HEAD_DIM)
    vblk = jnp.pad(v, pad).reshape(B, nb + 2, BLOCK, N_KV_HEADS, HEAD_DIM)
    kb = jnp.concatenate([kblk[:, :-2], kblk[:, 1:-1], kblk[:, 2:]], axis=2).astype(jnp.float32)
    vb = jnp.concatenate([vblk[:, :-2], vblk[:, 1:-1], vblk[:, 2:]], axis=2).astype(jnp.float32)
    qb = q.reshape(B, nb, BLOCK, N_KV_HEADS, Q_PER_KV, HEAD_DIM).astype(jnp.float32)
    qpos = jnp.arange(L).reshape(nb, BLOCK)
    kpos = (jnp.arange(nb)[:, None] - 1) * BLOCK + jnp.arange(3 * BLOCK)[None]
    valid = (jnp.abs(qpos[:, :, None] - kpos[:, None, :]) <= WINDOW) & ((kpos >= 0) & (kpos < L))[:, None, :]
    s_loc = jnp.einsum('bnqkgd,bnskd->bnkgqs', qb, kb) * ATTN_SCALE
    s_loc = jnp.where(valid[None, :, None, None], s_loc, NEG)
    s_ctx = jnp.einsum('bnqkgd,bpkd->bnkgqp', qb, ctx_k.astype(jnp.float32)) * ATTN_SCALE
    p = sink_softmax(jnp.concatenate([s_loc, s_ctx], axis=-1), sink.astype(jnp.float32)[None, None, :, :, None, None])
    o = (jnp.einsum('bnkgqs,bnskd->bnqkgd', p[..., :3 * BLOCK], vb)
         + jnp.einsum('bnkgqp,bpkd->bnqkgd', p[..., 3 * BLOCK:], ctx_v.astype(jnp.float32)))
    return o.reshape(B, L, ATTN_WIDTH).astype(q.dtype)


def even_branches(h, w_in, conv_w, conv_b, q_norm_g, k_norm_g):
    B, L, _ = h.shape
    z = h @ w_in
    bg, cg, xs, ga, q, k, v, gb = jnp.split(z, EVEN_SPLITS, axis=-1)
    ya = bg * short_conv(cg * xs, conv_w, conv_b) * jax.nn.silu(ga)
    q = rmsnorm(q.reshape(B, L, N_Q_HEADS, HEAD_DIM), q_norm_g)
    k = rmsnorm(k.reshape(B, L, N_KV_HEADS, HEAD_DIM), k_norm_g)
    v = v.reshape(B, L, N_KV_HEADS, HEAD_DIM)
    return ya, q, k, v, jax.nn.silu(gb)


def multiscale_pool(u):
    L = u.shape[1]
    uf = u.astype(jnp.float32)
    cs = jnp.pad(jnp.cumsum(uf, axis=1), ((0, 0), (1, 0), (0, 0)))
    t = jnp.arange(L)
    outs = []
    for gi, w in enumerate(POOL_SIZES):
        lo = jnp.clip(t - w // 2, 0, L)
        hi = jnp.clip(t + w // 2, 0, L)
        seg = cs[:, :, gi * POOL_GROUP:(gi + 1) * POOL_GROUP]
        outs.append((seg[:, hi] - seg[:, lo]) / (hi - lo).astype(jnp.float32)[None, :, None])
    return (jnp.concatenate(outs, axis=-1) - uf).astype(u.dtype)


def pool_mixer(h, w_in, pool_w, pool_scale, w_out):
    B, L, _ = h.shape
    u, g = jnp.split(h @ w_in, 2, axis=-1)
    y = multiscale_pool(u).reshape(B, L, N_POOL_GROUPS, POOL_GROUP)
    y = jnp.einsum('blgc,gcd->blgd', y, pool_w).reshape(B, L, POOL_WIDTH) * pool_scale
    return (y * jax.nn.silu(g)) @ w_out


def setup_inputs(seed: int = 0) -> dict:
    key = jax.random.key(seed)
    ks = jax.random.split(key, 24)
    nrm = jax.random.normal
    D = D_MODEL
    return {
        "x_prompt": nrm(ks[0], (BATCH, SEQ, D), jnp.float32),
        "x_sample": nrm(ks[1], (DEC_BATCH, DEC_SEQ, D), jnp.float32),
        "cache_k": nrm(ks[2], (DEC_BATCH, N_EVEN, PAST_LEN, N_KV_HEADS, HEAD_DIM), jnp.float32),
        "cache_v": nrm(ks[3], (DEC_BATCH, N_EVEN, PAST_LEN, N_KV_HEADS, HEAD_DIM), jnp.float32),
        "c": nrm(ks[4], (DEC_BATCH, D), jnp.float32),
        "c_ctx": nrm(ks[5], (D,), jnp.float32),
        "ada_w_e": nrm(ks[6], (N_EVEN, D, 3 * D), jnp.float32) * (0.5 * D ** -0.5),
        "ada_b_e": nrm(ks[7], (N_EVEN, 3 * D), jnp.float32) * 0.01,
        "norm_g_e": 1.0 + 0.01 * nrm(ks[8], (N_EVEN, D), jnp.float32),
        "w_in_e": nrm(ks[9], (N_EVEN, D, EVEN_IN), jnp.float32) * D ** -0.5,
        "conv_w": nrm(ks[10], (N_EVEN, 3, CONV_WIDTH), jnp.float32) * 0.5,
        "conv_b": nrm(ks[11], (N_EVEN, CONV_WIDTH), jnp.float32) * 0.01,
        "q_norm_g": 1.0 + 0.01 * nrm(ks[12], (N_EVEN, HEAD_DIM), jnp.float32),
        "k_norm_g": 1.0 + 0.01 * nrm(ks[13], (N_EVEN, HEAD_DIM), jnp.float32),
        "sink": nrm(ks[14], (N_EVEN, N_Q_HEADS), jnp.float32) * 0.5,
        "w_out_e": nrm(ks[15], (N_EVEN, D, D), jnp.float32) * D ** -0.5,
        "ada_w_o": nrm(ks[16], (N_ODD, D, 3 * D), jnp.float32) * (0.5 * D ** -0.5),
        "ada_b_o": nrm(ks[17], (N_ODD, 3 * D), jnp.float32) * 0.01,
        "norm_g_o": 1.0 + 0.01 * nrm(ks[18], (N_ODD, D), jnp.float32),
        "w_in_o": nrm(ks[19], (N_ODD, D, ODD_IN), jnp.float32) * D ** -0.5,
        "pool_w": nrm(ks[20], (N_ODD, N_POOL_GROUPS, POOL_GROUP, POOL_GROUP), jnp.float32) * POOL_GROUP ** -0.5,
        "pool_scale": 1.0 + 0.1 * nrm(ks[21], (N_ODD, POOL_WIDTH), jnp.float32),
        "w_out_o": nrm(ks[22], (N_ODD, POOL_WIDTH, D), jnp.float32) * POOL_WIDTH ** -0.5,
    }


def reference(x_prompt, x_sample, cache_k, cache_v, c, c_ctx,
              ada_w_e, ada_b_e, norm_g_e, w_in_e, conv_w, conv_b, q_norm_g, k_norm_g, sink, w_out_e,
              ada_w_o, ada_b_o, norm_g_o, w_in_o, pool_w, pool_scale, w_out_o):
    n_rows = x_sample.shape[1] // GRID_W
    cos, sin = axial_rope_tables(n_rows)
    yp, ys = x_prompt, x_sample
    new_k, new_v = [], []
    for layer in range(DEPTH):
        i = layer // 2
        if layer % 2 == 0:
            sp, scp, gp = adaln(c_ctx[None], ada_w_e[i], ada_b_e[i])
            ss, scs, gs = adaln(c, ada_w_e[i], ada_b_e[i])
            snk = sink[i].reshape(N_KV_HEADS, Q_PER_KV)
            hp = modulate(yp, norm_g_e[i], sp, scp)
            ya, q, k, v, gb = even_branches(hp, w_in_e[i], conv_w[i], conv_b[i], q_norm_g[i], k_norm_g[i])
            ob = context_attention(q, k, v, snk)
            yp = yp + gp * (jnp.concatenate([ya, ob * gb], axis=-1) @ w_out_e[i])
            new_k.append(k)
            new_v.append(v)
            hs = modulate(ys, norm_g_e[i], ss, scs)
            ya, q, k, v, gb = even_branches(hs, w_in_e[i], conv_w[i], conv_b[i], q_norm_g[i], k_norm_g[i])
            q = apply_axial_rope(q, cos, sin)
            k = apply_axial_rope(k, cos, sin)
            ob = latent_attention(q, k, v, cache_k[:, i], cache_v[:, i], snk)
            ys = ys + gs * (jnp.concatenate([ya, ob * gb], axis=-1) @ w_out_e[i])
        else:
            sp, scp, gp = adaln(c_ctx[None], ada_w_o[i], ada_b_o[i])
            ss, scs, gs = adaln(c, ada_w_o[i], ada_b_o[i])
            hp = modulate(yp, norm_g_o[i], sp, scp)
            yp = yp + gp * pool_mixer(hp, w_in_o[i], pool_w[i], pool_scale[i], w_out_o[i])
            hs = modulate(ys, norm_g_o[i], ss, scs)
            ys = ys + gs * pool_mixer(hs, w_in_o[i], pool_w[i], pool_scale[i], w_out_o[i])
    new_cache_k = jnp.stack(new_k, axis=1)
    new_cache_v = jnp.stack(new_v, axis=1)
    return (yp, ys, new_cache_k, new_cache_v)
```

```python
import os
import numpy as np
from contextlib import ExitStack
import concourse.bass as bass
import concourse.mybir as mybir
from concourse.bass_utils import run_bass_kernel_spmd

F32 = mybir.dt.float32
BF16 = mybir.dt.bfloat16
AF = mybir.ActivationFunctionType
ALU = mybir.AluOpType

D = 1024
EPS = 1e-6
NEGM = -30000.0
NDS = 16


class Prog:
    ENGS = ("pe", "act", "dve", "pool", "sp")

    def __init__(self):
        self.ops = []
        self.lastw = {}
        self.readers = {}
        self.bar_deps = set()
        self.bar_pending = set()

    def add(self, eng, fn, reads=(), writes=(), dma=False):
        i = len(self.ops)
        deps = set()
        for k in reads:
            w = self.lastw.get(k)
            if w is not None:
                deps.add(w)
            if k[0] == "ps":
                for rd in self.readers.get(k, ()):
                    if self.ops[rd]["eng"] != eng:
                        deps.add(rd)
        best = {}
        for k in writes:
            w = self.lastw.get(k)
            if w is not None:
                deps.add(w)
                assert not (k[0] == "wr" and not self.readers.get(k)), "dead weight load into %r" % (k,)
            for rd in self.readers.get(k, ()):
                o = self.ops[rd]
                if o["dma"]:
                    deps.add(rd)
                elif rd > best.get(o["eng"], -1):
                    best[o["eng"]] = rd
        deps |= set(best.values())
        for k in reads:
            self.readers.setdefault(k, set()).add(i)
        for k in writes:
            self.lastw[k] = i
            self.readers[k] = set()
        if eng in self.bar_pending:
            deps |= self.bar_deps
            self.bar_pending.discard(eng)
        deps.discard(i)
        self.ops.append(dict(eng=eng, fn=fn, deps=deps, dma=dma))
        return i

    def barrier(self):
        last = {}
        b = set()
        for i, o in enumerate(self.ops):
            if o["dma"]:
                b.add(i)
            else:
                last[o["eng"]] = i
        b |= set(last.values())
        self.bar_deps = b
        self.bar_pending = set(self.ENGS)

    def emit(self, nc, es):
        ops = self.ops
        esem = {e: es.enter_context(nc.semaphore("sem_" + e)) for e in self.ENGS}
        dsem = [es.enter_context(nc.semaphore("dsem%d" % i)) for i in range(NDS)]
        dcount = [0] * NDS
        dlast = [None] * NDS
        nd = {"sp": 0, "pool": 0}
        half = NDS // 2
        for i, o in enumerate(ops):
            if o["dma"]:
                q = o["eng"]
                s = (nd[q] % half) + (0 if q == "sp" else half)
                nd[q] += 1
                if dlast[s] is not None:
                    o["deps"].add(dlast[s])
                dcount[s] += 1
                o["dsem"] = s
                o["dval"] = 16 * dcount[s]
                dlast[s] = i
        sig = set()
        for i, o in enumerate(ops):
            for d in o["deps"]:
                p = ops[d]
                if p["dma"]:
                    continue
                if p["eng"] == "pe" and o["eng"] == "pe" and not o["dma"]:
                    continue
                sig.add(d)
        cnt = {e: 0 for e in self.ENGS}
        for i, o in enumerate(ops):
            if (not o["dma"]) and i in sig:
                cnt[o["eng"]] += 1
                o["sval"] = cnt[o["eng"]]
        final_d = list(dcount)
        block = es.enter_context(nc.Block())

        def run(eng_name, e):
            waited = {}
            for i, o in enumerate(ops):
                if o["eng"] != eng_name:
                    continue
                need = {}
                for d in o["deps"]:
                    p = ops[d]
                    if p["dma"]:
                        ch = ("d", p["dsem"])
                        v = p["dval"]
                    else:
                        if p["eng"] == "pe" and eng_name == "pe" and not o["dma"]:
                            continue
                        ch = ("e", p["eng"])
                        v = p["sval"]
                    if v > need.get(ch, 0):
                        need[ch] = v
                for ch, v in need.items():
                    if waited.get(ch, 0) >= v:
                        continue
                    waited[ch] = v
                    s = dsem[ch[1]] if ch[0] == "d" else esem[ch[1]]
                    e.wait_ge(s, v)
                ins = o["fn"](e)
                if o["dma"]:
                    ins.then_inc(dsem[o["dsem"]], 16)
                elif i in sig:
                    ins.then_inc(esem[eng_name], 1)
            if eng_name == "sp":
                for s in range(NDS):
                    if final_d[s] > 0:
                        e.wait_ge(dsem[s], 16 * final_d[s])

        @block.sync
        def _(e):
            run("sp", e)

        @block.gpsimd
        def _(e):
            run("pool", e)

        @block.scalar
        def _(e):
            run("act", e)

        @block.vector
        def _(e):
            run("dve", e)

        @block.tensor
        def _(e):
            run("pe", e)


HALO = 16
TILES = [(0, 512, 0), (512 + 128 + (128 - HALO), 384 + HALO, 512 + (128 - HALO)), (512 + 640, 512, 1024),
         (512 + 1152, 128 + HALO, 1536)]
NFULL = 1792
TILES_G = [(0, 512, 0), (512 + 256, 512, 512 + 128), (512 + 768, 512, 512 + 640)]
CXW = 1796
UW = 1888

C_COND, C_ABE, C_ABO, C_NGE, C_NGO, C_CW, C_CB, C_GQ, C_GK, C_SINK, C_PSC, C_VF, C_PCE, C_SCE = (
    0, 16, 40, 64, 72, 80, 92, 96, 97, 98, 102, 110, 128, 256)
NSMALL = 320
CB_ID, CB_PM, CB_BO, CB_NML, CB_NMR, CB_O3 = 0, 128, 256, 384, 640, 896
NCB = 1088


def build_program(stop=None):
    nc = bass.Bass("TRN2", target_bir_lowering=False)
    P = Prog()
    es = ExitStack()

    def din(name, shape):
        return nc.dram_tensor(name, list(shape), F32, kind="ExternalInput").ap()

    def dout(name, shape):
        return nc.dram_tensor(name, list(shape), F32, kind="ExternalOutput").ap()

    xp_d = din("xp", [4, 128, D])
    xs_d = din("xs", [12, 128, D])
    smalls_d = din("smalls", [128, NSMALL])
    gkrow_d = din("gkrow", [128, 64])
    tabc_d = din("tabc", [128, 1536])
    tabs_d = din("tabs", [128, 1536])
    cb_d = din("cb", [128, NCB])
    idf_d = din("idf", [128, 128])
    kc_d = din("kc", [128, 2, 256])
    cv_d = din("cv", [128, 2, 128])
    wada_e_d = din("wada_e", [D, 3072]).rearrange("(kc p) n -> p kc n", p=128)
    wada_o_d = din("wada_o", [D, 3072]).rearrange("(kc p) n -> p kc n", p=128)
    win_e_d = din("win_e", [D, 3584]).rearrange("(kc p) n -> p kc n", p=128)
    wout_e_d = din("wout_e", [D, D]).rearrange("(kc p) n -> p kc n", p=128)
    win_o_d = din("win_o", [D, 2048]).rearrange("(kc p) n -> p kc n", p=128)
    wout_o_d = din("wout_o", [D, D]).rearrange("(kc p) n -> p kc n", p=128)
    pw_d = din("pw", [4, 256, 256]).rearrange("g (cc p) n -> p g cc n", p=128)
    yp_d = dout("yp", [4, 128, D])
    ysm_d = dout("ysm", [8, 128, D])
    nk_d = dout("nk", [4, 128, 128])
    nv_d = dout("nv", [4, 128, 128])

    def finish():
        P.emit(nc, es)
        es.close()
        return nc

    def sb(name, shape, dt):
        return es.enter_context(nc.sbuf_tensor(name, list(shape), dt))

    BIG = sb("big", [128, 14336], F32)
    HT = sb("ht", [128, 8, 2048], BF16)
    AT = sb("at", [128, 8, NFULL], BF16)
    WR = sb("wr", [128, 3, 8, 512], BF16)
    FA = sb("fa", [128, 2048], F32)
    FB = sb("fb", [128, 2048], F32)
    FC = sb("fc", [128, UW], F32)
    GT = sb("gt", [128, D], F32)
    YT = sb("yt", [128, 2, NFULL], BF16)
    SGT = sb("sgt", [128, 2, NFULL], BF16)
    QG = sb("qg", [128, 512], BF16)
    SQ = sb("sq", [128, 512], BF16)
    QG2 = sb("qg2", [128, 512], BF16)
    SQ2 = sb("sq2", [128, 512], BF16)
    JUNK = sb("junk", [128, 1024], BF16)
    SM = sb("sm", [128, NSMALL], F32)
    CB = sb("cbs", [128, NCB], BF16)
    IDF = sb("idfs", [128, 128], F32)
    GKR = sb("gkr", [128, 64], F32)
    PW = sb("pws", [128, 4, 2, 256], BF16)
    ST = sb("st", [128, 16], BF16)
    MA = sb("ma", [128, 2, 48], F32)
    GS = sb("gs", [128, 2, 2, 8], F32)
    SS = sb("ssx", [128, 64], F32)
    RK = sb("rk", [128, 16, 2], F32)
    RKS = sb("rks", [128, 16, 2], F32)
    EST = sb("est", [128, 4], F32)
    VALX = sb("valx", [128, 4, 192], BF16)
    GBT = sb("gbt", [128, 128], F32)
    KO = FA[:, 0:512].rearrange("p (b c) -> p b c", b=4)
    VO = FA[:, 512:1024].rearrange("p (b c) -> p b c", b=4)
    RD = sb("rd", [128, 256], F32)
    TMP = sb("tmp", [128, 512], F32)
    TMP2 = sb("tmp2", [128, 512], F32)
    TMP3 = sb("tmp3", [128, 16], F32)
    PS = [es.enter_context(nc.psum_tensor("ps%d" % i, [128, 512], F32)) for i in range(8)]
    QGS_K = [(QG, ("qg",)), (QG2, ("qg2",))]

    def bigv(lo, n32):
        return BIG[:, lo:lo + n32]
    KK = bigv(0, 2048).bitcast(BF16).rearrange("p (k t) -> p k t", k=2)
    VV = bigv(2048, 3456).bitcast(BF16).rearrange("p (b k c) -> p b k c", b=18, k=2)
    QT = bigv(5504, 3584).bitcast(BF16).rearrange("p (c t) -> p c t", c=4)
    TABC = bigv(9088, 1536)
    TABS = bigv(10624, 1536)
    ET = bigv(12160, 1024).bitcast(BF16).rearrange("p (s t) -> p s t", s=4)
    KC = bigv(13184, 256).bitcast(BF16).rearrange("p (k t) -> p k t", k=2)

    def YS(b):
        return BIG[:, b * 1024:(b + 1) * 1024]
    XN = FB[:, :].bitcast(BF16).rearrange("p (s f) -> p s f", s=4)

    bank_ctr = [0]

    def banks(n=1):
        r = [(bank_ctr[0] + i) % 8 for i in range(n)]
        bank_ctr[0] = (bank_ctr[0] + n) % 8
        return r

    def kb(b):
        return ("ps", b)

    def sm(c0, c1=None):
        return SM[:, c0:(c0 + 1 if c1 is None else c1)]

    P.add("sp", lambda e: e.dma_start(out=SM[:, :], in_=smalls_d), writes=[("sm",)], dma=True)
    P.add("sp", lambda e: e.dma_start(out=IDF[:, :], in_=idf_d), writes=[("idf",)], dma=True)
    P.add("sp", lambda e: e.dma_start(out=GKR[:, :], in_=gkrow_d), writes=[("gkr",)], dma=True)
    P.add("sp", lambda e: e.dma_start(out=TABC, in_=tabc_d), writes=[("tab",)], dma=True)
    P.add("sp", lambda e: e.dma_start(out=TABS, in_=tabs_d), writes=[("tab",)], dma=True)
    P.add("pool", lambda e: e.dma_start(out=CB[:, :], in_=cb_d), writes=[("cb",)], dma=True)
    P.add("dve", lambda e: e.memset(VV[:, :, :, 64:128], 0.0), writes=[("vv", b) for b in range(18)])
    P.add("dve", lambda e: e.memset(SS[:, :], 0.0), writes=[("ss", i) for i in range(64)])
    P.add("dve", lambda e: e.memset(RK[:, :, :], 0.0), writes=[("rk", i) for i in range(16)])
    P.add("act", lambda e: e.activation(out=ST[:, :], in_=SM[:, C_COND:C_COND + 16], func=AF.Silu),
          reads=[("sm",)], writes=[("st",)])
    for i, wb in enumerate((0, 1, 10, 11)):
        P.add("dve", lambda e, i=i, wb=wb: e.tensor_scalar(out=VALX[:, i, :], in0=CB[:, CB_O3:CB_O3 + 192],
                                                            scalar1=SM[:, C_VF + wb:C_VF + wb + 1], scalar2=None,
                                                            op0=ALU.mult),
              reads=[("sm",), ("cb",)], writes=[("valx", i)])
    P.add("act", lambda e: e.activation(out=EST[:, :], in_=SM[:, C_SINK:C_SINK + 4], func=AF.Exp),
          reads=[("sm",)], writes=[("est",)])

    wr_ctr = [0]

    def load_w(dram, c0, ncols=512, slot=None):
        if slot is None:
            s = wr_ctr[0] % 3
            wr_ctr[0] += 1
        else:
            s = slot
        P.add("pool", lambda e: e.dma_start(out=WR[:, s, :, 0:ncols], in_=dram[:, :, c0:c0 + ncols]),
              writes=[("wr", s)], dma=True)
        return s

    def adaln_group(L, g, slot=None, bank=None):
        wd, cbias = ((wada_e_d, C_ABE), (wada_o_d, C_ABO))[L]
        bk = banks(1)[0] if bank is None else bank
        s = load_w(wd, g * 512, slot=slot)
        for oc4 in range(4):
            for kc in range(8):
                P.add("pe", lambda e, s=s, oc4=oc4, kc=kc, bk=bk: e.matmul(
                    PS[bk][:, oc4 * 2:oc4 * 2 + 2], lhsT=WR[:, s, kc, oc4 * 128:(oc4 + 1) * 128],
                    rhs=ST[:, kc * 2:kc * 2 + 2], start=(kc == 0), stop=(kc == 7)),
                    reads=[("wr", s), ("st",)], writes=[kb(bk)])
        third = (g * 4) // 8
        for c in range(2):
            P.add("dve", lambda e, L=L, bk=bk, cbias=cbias, c=c, g=g: e.tensor_tensor(
                out=MA[:, L, g * 8:(g + 1) * 8].rearrange("p (o c) -> p o c", c=2)[:, :, c],
                in0=PS[bk][:, 0:8].rearrange("p (o c) -> p o c", c=2)[:, :, c],
                in1=SM[:, cbias + g * 4:cbias + g * 4 + 4], op=ALU.add),
                reads=[kb(bk), ("sm",)], writes=[("ma", L, third, g % 2)])
        if g == 3:
            cng = C_NGE if L == 0 else C_NGO
            for c in range(2):
                P.add("dve", lambda e, L=L, c=c, cng=cng: e.scalar_tensor_tensor(
                    out=GS[:, L, c, :], in0=MA[:, L, :].rearrange("p (o c) -> p o c", c=2)[:, 8:16, c], scalar=1.0,
                    in1=SM[:, cng:cng + 8], op0=ALU.add, op1=ALU.mult),
                    reads=[("ma", L, 1, 0), ("ma", L, 1, 1), ("sm",)], writes=[("gs", L, c)])

    for g in range(4):
        adaln_group(0, g)
    P.add("pool", lambda e: e.dma_start(out=PW[:, :, :, :], in_=pw_d), writes=[("pw",)], dma=True)
    P.add("pool", lambda e: e.dma_start(out=KC, in_=kc_d), writes=[("kc",)], dma=True)
    for half in (0, 128):
        P.add("pool", lambda e, half=half: e.dma_start(out=VV[:, 16:18, :, half:half + 64],
                                                         in_=cv_d.rearrange("p b (k d) -> p b k d", k=2)),
              writes=[("vv", 16), ("vv", 17)], dma=True)
    deferred = [(0, 4), (0, 5)] + [(1, g) for g in range(6)]

    def run_deferred(n=1, slot=None, bank=None):
        for _ in range(n):
            if deferred:
                adaln_group(*deferred.pop(0), slot=slot, bank=bank)

    def shiftv(L, c, kc):
        return MA[:, L, kc * 2 + c:kc * 2 + c + 1]

    def gatev(L, c, kc):
        return MA[:, L, (16 + kc) * 2 + c:(16 + kc) * 2 + c + 1]

    if stop == 'adaln':
        return finish()
    tog = [0]

    def phase1(L, groups):
        for grp in groups:
            phase1_b(phase1_a(L, grp))

    def phase1_a(L, grp):
        cond, blks = grp
        if True:
            for j, (getter, hb) in enumerate(blks):
                src, skeys = getter()
                col = (tog[0] % 32)
                tog[0] += 1
                P.add("act", lambda e, src=src, col=col: e.activation(out=JUNK[:, :], in_=src, func=AF.Square,
                                                                      accum_out=SS[:, col:col + 1]),
                      reads=skeys, writes=[("junk",), ("ss", col)])
                P.add("act", lambda e, col=col: e.activation(out=SS[:, 32 + col:33 + col], in_=SS[:, col:col + 1],
                                                              func=AF.Sqrt, scale=1.0 / D, bias=SM[:, 127:128]),
                      reads=[("ss", col), ("sm",)], writes=[("ss", 32 + col)])
                P.add("dve", lambda e, col=col: e.reciprocal(out=SS[:, 32 + col:33 + col], in_=SS[:, 32 + col:33 + col]),
                      reads=[("ss", 32 + col)], writes=[("ss", 32 + col)])
                P.add("dve", lambda e, src=src, col=col, j=j: e.tensor_scalar(
                    out=XN[:, j, :], in0=src, scalar1=SS[:, 32 + col:33 + col], scalar2=None, op0=ALU.mult),
                    reads=list(skeys) + [("ss", 32 + col)], writes=[("fb", j)])
        return (L, cond, blks)

    def phase1_b(state):
        L, cond, blks = state
        if True:
            bks = banks(4)
            for j, (getter, hb) in enumerate(blks):
                for kc in range(8):
                    bk = bks[kc // 2]
                    off = (kc % 2) * 256 + j * 64
                    P.add("pe", lambda e, bk=bk, off=off, j=j, kc=kc: e.transpose(
                        PS[bk][:, off:off + 64].bitcast(BF16), XN[:, j, kc * 128:(kc + 1) * 128],
                        CB[:, CB_ID:CB_ID + 128]),
                        reads=[("fb", j), ("cb",)], writes=[kb(bk)])
            n = len(blks)
            hb0 = blks[0][1]
            for kc in range(8):
                bk = bks[kc // 2]
                src = PS[bk][:, (kc % 2) * 256:(kc % 2) * 256 + n * 64].bitcast(BF16)
                dst = HT[:, kc, hb0 * 128:(hb0 + n) * 128]
                wk = [("ht", hb0 + i) for i in range(n)]
                if kc % 4 == 0:
                    P.add("act", lambda e, src=src, dst=dst, kc=kc, cond=cond: e.activation(
                        out=dst, in_=src, func=AF.Identity, scale=GS[:, L, cond, kc:kc + 1], bias=shiftv(L, cond, kc)),
                        reads=[kb(bk), ("gs", L, cond), ("ma", L, 0, 0), ("ma", L, 0, 1)], writes=wk)
                else:
                    P.add("dve", lambda e, src=src, dst=dst, kc=kc, cond=cond: e.tensor_scalar(
                        out=dst, in0=src, scalar1=GS[:, L, cond, kc:kc + 1], scalar2=shiftv(L, cond, kc),
                        op0=ALU.mult, op1=ALU.add),
                        reads=[kb(bk), ("gs", L, cond), ("ma", L, 0, 0), ("ma", L, 0, 1)], writes=wk)

    def zero_invalid(wbs):
        for wb in wbs:
            hb = 4 + wb
            P.add("dve", lambda e, wb=wb, hb=hb: e.tensor_scalar(
                out=HT[:, :, hb * 128:(hb + 1) * 128], in0=HT[:, :, hb * 128:(hb + 1) * 128],
                scalar1=SM[:, C_VF + wb:C_VF + wb + 1], scalar2=None, op0=ALU.mult),
                reads=[("ht", hb), ("sm",)], writes=[("ht", hb)])

    xr = [0]

    ATX = AT[:, :, :].rearrange("p a b -> p (a b)").bitcast(F32)
    NXS = 4

    def atx_keys(sl):
        ks = set()
        for col in range(sl * 2048, (sl + 1) * 2048, 128):
            ks.add(("at", col // NFULL, (col % NFULL) // 128))
            ks.add(("at", (col + 127) // NFULL, ((col + 127) % NFULL) // 128))
        return sorted(ks)

    def xsrc(dram_blk):
        s = xr[0] % NXS
        xr[0] += 1
        keys = atx_keys(s)
        P.add("sp", lambda e: e.dma_start(out=ATX[:, s * 1024:(s + 1) * 1024], in_=dram_blk),
              writes=keys, dma=True)
        return ATX[:, s * 1024:(s + 1) * 1024], keys

    for cond, specs in ((0, [("p", i) for i in range(4)]),
                        (1, [("s", i) for i in range(0, 4)]),
                        (1, [("s", i) for i in range(4, 8)]),
                        (1, [("s", i) for i in range(8, 12)])):
        blks = []
        for kind, i in specs:
            blks.append(((lambda kind=kind, i=i: xsrc(xp_d[i] if kind == "p" else xs_d[i])),
                         i if kind == "p" else 4 + i))
        phase1(0, [(cond, blks)])
    zero_invalid((0, 1, 10, 11))

    if stop == 'phase1':
        return finish()
    def proj(wslot, wc0, tiles, bks, tile_outer=False):
        if tile_outer:
            for (h0, n, f0), bk in zip(tiles, bks):
                for kc in range(8):
                    P.add("pe", lambda e, kc=kc, h0=h0, n=n, bk=bk: e.matmul(
                        PS[bk][:, 0:n], lhsT=WR[:, wslot, kc, wc0:wc0 + 128], rhs=HT[:, kc, h0:h0 + n],
                        start=(kc == 0), stop=(kc == 7)),
                        reads=[("wr", wslot)] + [("ht", b_) for b_ in range(h0 // 128, (h0 + n - 1) // 128 + 1)],
                        writes=[kb(bk)])
            return
        for _ in proj_slabs(wslot, wc0, tiles, bks):
            pass

    def proj_slabs(wslot, wc0, tiles, bks):
        for kc in range(8):
            yield_after = True
            for (h0, n, f0), bk in zip(tiles, bks):
                P.add("pe", lambda e, kc=kc, h0=h0, n=n, bk=bk: e.matmul(
                    PS[bk][:, 0:n], lhsT=WR[:, wslot, kc, wc0:wc0 + 128], rhs=HT[:, kc, h0:h0 + n],
                    start=(kc == 0), stop=(kc == 7)),
                    reads=[("wr", wslot)] + [("ht", b_) for b_ in range(h0 // 128, (h0 + n - 1) // 128 + 1)],
                    writes=[kb(bk)])
            yield kc

    def fkeys(name, c, f0, n):
        return [(name, c, b_) for b_ in range(f0 // 128, (f0 + n - 1) // 128 + 1)]

    s_kv = load_w(win_e_d, 6 * 512)
    for hb in range(16):
        bk = banks(1)[0]
        for kc in range(8):
            P.add("pe", lambda e, kc=kc, hb=hb, bk=bk: e.matmul(
                PS[bk][:, 0:256], lhsT=HT[:, kc, hb * 128:(hb + 1) * 128], rhs=WR[:, s_kv, kc, 0:256],
                start=(kc == 0), stop=(kc == 7)),
                reads=[("wr", s_kv), ("ht", hb)], writes=[kb(bk)])
        for h in range(2):
            P.add("act", lambda e, h=h, hb=hb, bk=bk: e.activation(
                out=JUNK[:, 0:64], in_=PS[bk][:, h * 64:(h + 1) * 64], func=AF.Square, accum_out=RK[:, hb, h:h + 1]),
                reads=[kb(bk)], writes=[("junk",), ("rk", hb)])
        P.add("act", lambda e, hb=hb: e.activation(out=RK[:, hb, :], in_=RK[:, hb, :], func=AF.Sqrt,
                                                    scale=1.0 / 64, bias=SM[:, 127:128]),
              reads=[("rk", hb), ("sm",)], writes=[("rk", hb)])
        P.add("dve", lambda e, hb=hb: e.reciprocal(out=RK[:, hb, :], in_=RK[:, hb, :]),
              reads=[("rk", hb)], writes=[("rk", hb)])
        P.add("dve", lambda e, hb=hb: e.tensor_scalar(out=RKS[:, hb, :], in0=RK[:, hb, :], scalar1=0.125,
                                                       scalar2=None, op0=ALU.mult),
              reads=[("rk", hb)], writes=[("rks", hb)])
        for half in (0, 128):
            P.add("dve", lambda e, half=half, hb=hb, bk=bk: e.tensor_copy(
                out=VV[:, hb, :, half:half + 64], in_=PS[bk][:, 128:256].rearrange("p (k d) -> p k d", k=2)),
                reads=[kb(bk)], writes=[("vv", hb)])
        if hb < 4:
            for h in range(2):
                P.add("dve", lambda e, h=h, hb=hb, bk=bk: e.scalar_tensor_tensor(
                    out=KO[:, hb, h * 64:(h + 1) * 64], in0=PS[bk][:, h * 64:(h + 1) * 64],
                    scalar=RK[:, hb, h:h + 1], in1=GKR[:, :], op0=ALU.mult, op1=ALU.mult),
                    reads=[kb(bk), ("rk", hb), ("gkr",)], writes=[("fa", 0)])
            P.add("dve", lambda e, hb=hb, bk=bk: e.tensor_copy(out=VO[:, hb, :], in_=PS[bk][:, 128:256]),
                  reads=[kb(bk)], writes=[("fa", 1)])
            P.add("sp", lambda e, hb=hb: e.dma_start(out=nk_d[hb], in_=KO[:, hb, :]), reads=[("fa", 0)], dma=True)
            P.add("sp", lambda e, hb=hb: e.dma_start(out=nv_d[hb], in_=VO[:, hb, :]), reads=[("fa", 1)], dma=True)
    ktiles = [(0, 512), (512, 512), (1024, 512), (1536, 512)]
    kk_items = [(kap, h0, n) for kap in range(2) for (h0, n) in ktiles]
    kk_banks = [banks(1)[0] for _ in kk_items]

    def kk_proj(i):
        kap, h0, n = kk_items[i]
        bk = kk_banks[i]
        for kc in range(8):
            P.add("pe", lambda e, kc=kc, h0=h0, n=n, bk=bk, kap=kap: e.matmul(
                PS[bk][:, 0:n], lhsT=WR[:, s_kv, kc, 256 + kap * 128:256 + (kap + 1) * 128],
                rhs=HT[:, kc, h0:h0 + n], start=(kc == 0), stop=(kc == 7)),
                reads=[("wr", s_kv)] + [("ht", h0 // 128 + i2) for i2 in range(4)], writes=[kb(bk)])

    kk_proj(0)
    for i, (kap, h0, n) in enumerate(kk_items):
        bk = kk_banks[i]
        kkw = [("kk", kap, h0 // 128 + i2) for i2 in range(4)]
        if h0 == 0:
            P.add("act", lambda e, bk=bk, kap=kap: e.activation(
                out=KK[:, kap, 0:512], in_=PS[bk][:, 0:512], func=AF.Identity, scale=SM[:, C_GK:C_GK + 1]),
                reads=[kb(bk), ("sm",)], writes=kkw)
            if i + 1 < len(kk_items):
                kk_proj(i + 1)
        else:
            t0 = h0 - 512
            qg, kqg = QGS_K[i % 2]
            P.add("act", lambda e, bk=bk, qg=qg: e.activation(
                out=qg[:, :], in_=PS[bk][:, 0:512], func=AF.Identity, scale=SM[:, C_GK:C_GK + 1]),
                reads=[kb(bk), ("sm",)], writes=[kqg])
            if i + 1 < len(kk_items):
                kk_proj(i + 1)
            P.add("pe", lambda e, bk=bk, qg=qg: e.matmul(PS[bk][:, 0:512], lhsT=CB[:, CB_PM:CB_PM + 128], rhs=qg[:, :],
                                                  start=True, stop=True),
                  reads=[("cb",), kqg], writes=[kb(bk)])
            P.add("dve", lambda e, bk=bk, t0=t0: e.tensor_tensor(out=FC[:, 0:512], in0=PS[bk][:, 0:512],
                                                                 in1=TABS[:, t0:t0 + 512], op=ALU.mult),
                  reads=[kb(bk), ("tab",)], writes=[("fc", 0)])
            P.add("dve", lambda e, t0=t0, qg=qg: e.tensor_tensor(out=FC[:, 512:1024], in0=qg[:, :],
                                                          in1=TABC[:, t0:t0 + 512], op=ALU.mult),
                  reads=[kqg, ("tab",)], writes=[("fc", 1)])
            P.add("dve", lambda e, kap=kap, h0=h0: e.tensor_tensor(out=KK[:, kap, h0:h0 + 512], in0=FC[:, 0:512],
                                                                  in1=FC[:, 512:1024], op=ALU.add),
                  reads=[("fc", 0), ("fc", 1)], writes=kkw)

    if stop == 'kv':
        return finish()
    for c0_, c1_ in ((512, 512 + 128 - HALO), (1536 + 128 + HALO, NFULL)):
        P.add("dve", lambda e, c0_=c0_, c1_=c1_: e.memset(AT[:, :, c0_:c1_], 0.0),
              writes=[("at", c_, b_) for c_ in range(8) for b_ in range(c0_ // 128, (c1_ - 1) // 128 + 1)])
    CX = FA[:, 0:CXW]
    ACC = FB[:, 0:CXW]
    P.add("dve", lambda e: e.memset(FA[:, :], 0.0), writes=[("fa", i) for i in range(4)])

    def cseg(buf, ti):
        if ti == 0:
            return buf[:, 1:515].rearrange("p (s c) -> p s c", c=257)[:, :, 0:256]
        f0 = TILES[ti][2] - 512
        n = TILES[ti][1]
        return buf[:, 515 + f0:515 + f0 + n]

    def pseg(bk, ti):
        if ti == 0:
            return PS[bk][:, 0:512].rearrange("p (s c) -> p s c", c=256)
        return PS[bk][:, 0:TILES[ti][1]]

    fa_all = [("fa", i) for i in range(4)]
    fb_all = [("fb", i) for i in range(4)]
    for j in range(4):
        run_deferred(1)
        s = load_w(win_e_d, j * 512)
        bks = banks(4)
        proj(s, 128, TILES, bks)
        for ti in range(4):
            P.add("act", lambda e, ti=ti, bk=bks[ti]: e.activation(out=cseg(CX, ti), in_=pseg(bk, ti), func=AF.Copy),
                  reads=[kb(bks[ti])], writes=fa_all)
        bks = banks(4)
        proj(s, 256, TILES, bks)
        for ti in range(4):
            P.add("dve", lambda e, ti=ti, bk=bks[ti]: e.tensor_tensor(out=cseg(CX, ti), in0=cseg(CX, ti),
                                                                      in1=pseg(bk, ti), op=ALU.mult),
                  reads=[kb(bks[ti])] + fa_all, writes=fa_all)
        P.add("dve", lambda e, j=j: e.tensor_scalar(out=ACC[:, 1:CXW - 1], in0=CX[:, 0:CXW - 2],
                                                    scalar1=SM[:, C_CW + j:C_CW + j + 1], scalar2=SM[:, C_CB + j:C_CB + j + 1],
                                                    op0=ALU.mult, op1=ALU.add),
              reads=fa_all + [("sm",)], writes=fb_all)
        P.add("dve", lambda e, j=j: e.scalar_tensor_tensor(out=ACC[:, 1:CXW - 1], in0=CX[:, 1:CXW - 1],
                                                           scalar=SM[:, C_CW + 4 + j:C_CW + 5 + j], in1=ACC[:, 1:CXW - 1],
                                                           op0=ALU.mult, op1=ALU.add),
              reads=fa_all + fb_all + [("sm",)], writes=fb_all)
        P.add("dve", lambda e, j=j: e.scalar_tensor_tensor(out=ACC[:, 1:CXW - 1], in0=CX[:, 2:CXW],
                                                           scalar=SM[:, C_CW + 8 + j:C_CW + 9 + j], in1=ACC[:, 1:CXW - 1],
                                                           op0=ALU.mult, op1=ALU.add),
              reads=fa_all + fb_all + [("sm",)], writes=fb_all)
        bks = banks(4)
        proj(s, 0, TILES, bks)
        for ti in range(4):
            P.add("dve", lambda e, ti=ti, bk=bks[ti]: e.tensor_tensor(out=cseg(ACC, ti), in0=cseg(ACC, ti),
                                                                      in1=pseg(bk, ti), op=ALU.mult),
                  reads=[kb(bks[ti])] + fb_all, writes=fb_all)
        bks = banks(4)
        proj(s, 384, TILES, bks)
        for ti, (h0, n, f0) in enumerate(TILES):
            P.add("act", lambda e, ti=ti, bk=bks[ti], n=n, f0=f0: e.activation(
                out=SGT[:, 0, f0:f0 + n], in_=PS[bk][:, 0:n], func=AF.Silu),
                reads=[kb(bks[ti])], writes=fkeys("sgt", 0, f0, n))
        for ti, (h0, n, f0) in enumerate(TILES):
            if ti == 0:
                o = AT[:, j, 0:512].rearrange("p (s c) -> p s c", c=256)
                i1 = SGT[:, 0, 0:512].rearrange("p (s c) -> p s c", c=256)
            else:
                o = AT[:, j, f0:f0 + n]
                i1 = SGT[:, 0, f0:f0 + n]
            P.add("dve", lambda e, ti=ti, o=o, i1=i1: e.tensor_tensor(out=o, in0=cseg(ACC, ti), in1=i1, op=ALU.mult),
                  reads=fb_all + fkeys("sgt", 0, f0, n), writes=fkeys("at", j, f0, n))

    if stop == 'conv':
        return finish()
    s_q = load_w(win_e_d, 4 * 512)
    s_gb = load_w(win_e_d, 5 * 512)
    assert s_q != s_gb
    s_free = 3 - s_q - s_gb
    RSETS = [(FC[:, 0:512], FC[:, 512:1024], FC[:, 1024:1536], ("fc", 0), ("fc", 1), ("fc", 2)),
             (FA[:, 0:512], FA[:, 512:1024], FA[:, 1024:1536], ("fa", 0), ("fa", 1), ("fa", 2))]
    QGS = [(QG, ("qg",)), (QG2, ("qg2",))]
    SQS = [(SQ, ("sq",)), (SQ2, ("sq2",))]
    tctr_q = [0]

    def q_post(c, bks):
        for _ in q_post_gen(c, bks):
            pass

    def q_post_gen(c, bks):
        ctx = []
        for ti, (h0, n, f0) in enumerate(TILES):
            k2 = tctr_q[0] % 2
            tctr_q[0] += 1
            ctx.append((ti, h0, n, f0, bks[ti], RSETS[k2], QGS[k2], SQS[k2]))

        def stage_a(t):
            ti, h0, n, f0, bk, (R1, R2, R3, kr1, kr2, kr3), (qg, kqg), (sq, ksq) = ctx[t]
            P.add("act", lambda e: e.activation(out=sq[:, 0:n], in_=PS[bk][:, 0:n], func=AF.Square),
                  reads=[kb(bk)], writes=[ksq])
            P.add("act", lambda e: e.activation(out=qg[:, 0:n], in_=PS[bk][:, 0:n], func=AF.Identity,
                                                scale=SM[:, C_GQ:C_GQ + 1]),
                  reads=[kb(bk), ("sm",)], writes=[kqg])

        def stage_ssq(t):
            ti, h0, n, f0, bk, (R1, R2, R3, kr1, kr2, kr3), (qg, kqg), (sq, ksq) = ctx[t]
            P.add("pe", lambda e: e.matmul(PS[bk][:, 0:n], lhsT=CB[:, CB_BO:CB_BO + 128], rhs=sq[:, 0:n],
                                           start=True, stop=True),
                  reads=[("cb",), ksq], writes=[kb(bk)])

        def stage_rstd(t):
            ti, h0, n, f0, bk, (R1, R2, R3, kr1, kr2, kr3), (qg, kqg), (sq, ksq) = ctx[t]
            P.add("act", lambda e: e.activation(out=R1[:, 0:n], in_=PS[bk][:, 0:n], func=AF.Ln,
                                                scale=1.0 / 64, bias=SM[:, 127:128]),
                  reads=[kb(bk), ("sm",)], writes=[kr1])
            P.add("act", lambda e: e.activation(out=R1[:, 0:n], in_=R1[:, 0:n], func=AF.Exp, scale=-0.5),
                  reads=[kr1], writes=[kr1])

        def stage_pm(t):
            ti, h0, n, f0, bk, (R1, R2, R3, kr1, kr2, kr3), (qg, kqg), (sq, ksq) = ctx[t]
            if ti != 0:
                P.add("pe", lambda e: e.matmul(PS[bk][:, 0:n], lhsT=CB[:, CB_PM:CB_PM + 128], rhs=qg[:, 0:n],
                                               start=True, stop=True),
                      reads=[("cb",), kqg], writes=[kb(bk)])

        def stage_dve(t):
            ti, h0, n, f0, bk, (R1, R2, R3, kr1, kr2, kr3), (qg, kqg), (sq, ksq) = ctx[t]
            qk = fkeys("qt", c, f0, n)
            if ti == 0:
                P.add("dve", lambda e: e.tensor_tensor(out=QT[:, c, f0:f0 + n], in0=qg[:, 0:n], in1=R1[:, 0:n],
                                                       op=ALU.mult),
                      reads=[kqg, kr1], writes=qk)
                return
            t0 = h0 - 512
            P.add("dve", lambda e: e.tensor_tensor(out=R2[:, 0:n], in0=PS[bk][:, 0:n], in1=TABS[:, t0:t0 + n],
                                                   op=ALU.mult),
                  reads=[kb(bk), ("tab",)], writes=[kr2])
            P.add("dve", lambda e: e.tensor_tensor(out=R3[:, 0:n], in0=qg[:, 0:n], in1=TABC[:, t0:t0 + n], op=ALU.mult),
                  reads=[kqg, ("tab",)], writes=[kr3])
            P.add("dve", lambda e: e.tensor_tensor(out=R3[:, 0:n], in0=R3[:, 0:n], in1=R2[:, 0:n], op=ALU.add),
                  reads=[kr2, kr3], writes=[kr3])
            P.add("dve", lambda e: e.tensor_tensor(out=QT[:, c, f0:f0 + n], in0=R3[:, 0:n], in1=R1[:, 0:n],
                                                   op=ALU.mult),
                  reads=[kr3, kr1], writes=qk)

        stage_a(0)
        yield "a"
        stage_ssq(0)
        yield "b"
        for t in range(4):
            if t + 1 < 4:
                stage_a(t + 1)
                yield "a"
            stage_rstd(t)
            stage_pm(t)
            yield "c"
            if t + 1 < 4:
                stage_ssq(t + 1)
                yield "b"
            stage_dve(t)

    def gb_post(c, bks):
        for ti, (h0, n, f0) in enumerate(TILES):
            P.add("act", lambda e, bk=bks[ti], n=n, f0=f0, c=c: e.activation(
                out=AT[:, 4 + c, f0:f0 + n], in_=PS[bk][:, 0:n], func=AF.Silu),
                reads=[kb(bks[ti])], writes=fkeys("at", 4 + c, f0, n))

    SETA, SETB = [0, 1, 2, 3], [4, 5, 6, 7]
    for c in range(4):
        bset = SETA if c % 2 == 0 else SETB
        proj(s_gb, c * 128, TILES, bset)
        gb_post(c, bset)
    proj(s_q, 0, TILES, SETA)
    for c in range(4):
        cur_set = SETA if c % 2 == 0 else SETB
        oth_set = SETB if c % 2 == 0 else SETA
        slabs = proj_slabs(s_q, (c + 1) * 128, TILES, oth_set) if c < 3 else iter(())
        for _ in q_post_gen(c, cur_set):
            next(slabs, None)
        for _ in slabs:
            pass
        run_deferred(1, slot=s_free, bank=cur_set[0])
    bank_ctr[0] = 0

    if stop == 'q':
        return finish()
    P.add("dve", lambda e: e.memset(GBT[:, :], 1.0), writes=[("gbt",)])

    def gate_tile(L, cond):
        dst, dkeys = (GT, [("gt",)]) if cond == 0 else (FC[:, 0:1024], [("fc", 0), ("fc", 1)])
        DG = FB[:, 0:1024]
        for kc in range(8):
            P.add("dve", lambda e, kc=kc: e.tensor_scalar(out=DG[:, kc * 128:(kc + 1) * 128], in0=IDF[:, :],
                                                          scalar1=gatev(L, cond, kc), scalar2=None, op0=ALU.mult),
                  reads=[("idf",), ("ma", L, 2, 0), ("ma", L, 2, 1)], writes=[("fb", kc // 4)])
        bks = banks(2)
        for h in range(2):
            P.add("pe", lambda e, h=h, bk=bks[h]: e.matmul(PS[bk][:, :], lhsT=GBT[:, :], rhs=DG[:, h * 512:(h + 1) * 512],
                                                           start=True, stop=True),
                  reads=[("gbt",), ("fb", h)], writes=[kb(bks[h])])
            P.add("act", lambda e, h=h, bk=bks[h], dst=dst: e.activation(out=dst[:, h * 512:(h + 1) * 512], in_=PS[bk][:, :],
                                                                func=AF.Copy),
                  reads=[kb(bks[h])], writes=dkeys)
        return dst, dkeys

    gts_l0 = {0: gate_tile(0, 0), 1: gate_tile(0, 1)}

    DBG = os.environ.get("ATT_DBG", "")
    items = []
    for qb in range(14):
        if qb < 4:
            s0 = (qb // 2) * 2
            klist = [(s0, s0, None, None), (s0 + 1, s0 + 1, None, None)]
            fq = qb * 128
        else:
            wb = qb - 3
            fq = 512 + (wb - 1) * 128

            def vx(w):
                return {0: 0, 1: 1, 10: 2, 11: 3}.get(w)
            klist = [(4 + wb - 1, 4 + wb - 1, CB_NML, vx(wb - 1)), (4 + wb, 4 + wb, None, vx(wb)),
                     (4 + wb + 1, 4 + wb + 1, CB_NMR, vx(wb + 1)), ("c", 16, None, None), ("c", 17, None, None)]
        for kap in range(2):
            gi = qb * 2 + kap
            for ki, kl in enumerate(klist):
                q0, nq = (112, 16) if qb == 4 else ((0, 16) if qb == 13 else (0, 128))
                items.append(dict(qb=qb, kap=kap, fq=fq, ki=ki, nk=len(klist), kl=kl, ob=gi % 2, n=len(items),
                                  q0=q0, nq=nq))

    grp_items = {}
    for it in items:
        grp_items.setdefault((it["qb"], it["kap"]), []).append(it)
    pgs = [k for k in grp_items if k[0] < 4]
    sgs = [k for k in grp_items if k[0] >= 4]
    order = []
    pi = 0
    for si, k in enumerate(sgs):
        order.append(k)
        if si % 2 == 1 and pi < len(pgs):
            order.append(pgs[pi])
            pi += 1
    order += pgs[pi:]
    items = []
    for gi, k in enumerate(order):
        for it in grp_items[k]:
            it = dict(it)
            it["ob"] = gi % 2
            it["n"] = len(items)
            items.append(it)

    def emit_S(it):
        n, kap, fq, q0, nq = it["n"], it["kap"], it["fq"], it["q0"], it["nq"]
        W = 2 * nq
        kblk, vblk, mcol, vxi = it["kl"]
        sp_ = 2 + 2 * (n % 3)
        sbanks = (sp_, sp_ + 1)
        masked = mcol is not None
        if masked:
            mrhs = CB[:, mcol:mcol + 256] if nq == 128 else \
                CB[:, mcol:mcol + 256].rearrange("p (j q) -> p j q", j=2)[:, :, q0:q0 + nq]
            for hb_ in range(2):
                P.add("pe", lambda e, sbk=sbanks[hb_], mrhs=mrhs: e.matmul(
                    PS[sbk][:, 0:W], lhsT=CB[:, CB_ID:CB_ID + 128], rhs=mrhs, start=True, stop=False),
                    reads=[("cb",)], writes=[kb(sbanks[hb_])])
        for hb_ in range(2):
            base = hb_ * 64
            sbk = sbanks[hb_]
            if kblk == "c":
                kt = KC[base:base + 64, kap, (vblk - 16) * 128:(vblk - 15) * 128]
                kr = [("kc",)]
            else:
                kt = KK[base:base + 64, kap, kblk * 128:(kblk + 1) * 128]
                kr = [("kk", kap, kblk)]
            sout = PS[sbk][:, 0:256].rearrange("p (j q) -> p j q", j=2) if nq == 128 else PS[sbk][:, 0:W]
            P.add("pe", lambda e, sout=sout, kt=kt, base=base, kap=kap, masked=masked: e.matmul(
                sout, lhsT=kt, rhs=QT[base:base + 64, 2 * kap:2 * kap + 2, fq + q0:fq + q0 + nq],
                start=(not masked), stop=True),
                reads=kr + [("qt", 2 * kap, fq // 128), ("qt", 2 * kap + 1, fq // 128)], writes=[kb(sbk)])
        es_ = n % 4
        for hb_ in range(2):
            sbk = sbanks[hb_]
            if kblk == "c":
                P.add("act", lambda e, sbk=sbk, es_=es_, hb_=hb_: e.activation(
                    out=ET[:, es_, hb_ * 256:hb_ * 256 + W], in_=PS[sbk][:, 0:W], func=AF.Exp, scale=0.125),
                    reads=[kb(sbk)], writes=[("et", es_, hb_)])
            else:
                P.add("act", lambda e, sbk=sbk, es_=es_, kblk=kblk, kap=kap, hb_=hb_: e.activation(
                    out=ET[:, es_, hb_ * 256:hb_ * 256 + W], in_=PS[sbk][:, 0:W], func=AF.Exp,
                    scale=RKS[:, kblk, kap:kap + 1]),
                    reads=[kb(sbk), ("rks", kblk)], writes=[("et", es_, hb_)])

    def emit_PV(it):
        n, kap, fq, ob, ki, q0, nq = it["n"], it["kap"], it["fq"], it["ob"], it["ki"], it["q0"], it["nq"]
        W = 2 * nq
        kblk, vblk, mcol, vxi = it["kl"]
        es_ = n % 4
        for half in range(2):
            for which in (0, 1):
                if which == 0:
                    lt = VV[:, vblk, kap, half * 64:half * 64 + 128]
                    lr = [("vv", vblk)]
                elif vxi is not None:
                    lt = VALX[:, vxi, half * 64:half * 64 + 128]
                    lr = [("valx", vxi)]
                else:
                    lt = CB[:, CB_O3 + half * 64:CB_O3 + half * 64 + 128]
                    lr = [("cb",)]
                stt = (ki == 0) and (half == 0) and (which == 0)
                stp = (ki == it["nk"] - 1) and (half == 1) and (which == 1)
                P.add("pe", lambda e, ob=ob, lt=lt, es_=es_, half=half, which=which, stt=stt, stp=stp: e.matmul(
                    PS[ob][:, which * 256:which * 256 + W], lhsT=lt, rhs=ET[:, es_, half * 256:half * 256 + W],
                    start=stt, stop=stp),
                    reads=lr + [("et", es_, half)], writes=[kb(ob)])
        if ki != it["nk"] - 1:
            return
        if it["nk"] == 2:
            for jj in range(2):
                c = 2 * kap + jj
                P.add("act", lambda e, ob=ob, jj=jj, c=c: e.activation(
                    out=RD[:, jj * nq:(jj + 1) * nq], in_=PS[ob][:, 256 + jj * nq:256 + (jj + 1) * nq],
                    func=AF.Ln, bias=EST[:, c:c + 1]),
                    reads=[kb(ob), ("est",)], writes=[("rd",)])
            P.add("act", lambda e: e.activation(out=RD[:, 0:W], in_=RD[:, 0:W], func=AF.Exp, scale=-1.0),
                  reads=[("rd",)], writes=[("rd",)])
        else:
            for jj in range(2):
                c = 2 * kap + jj
                P.add("dve", lambda e, ob=ob, jj=jj, c=c: e.tensor_scalar(
                    out=RD[:, jj * nq:(jj + 1) * nq], in0=PS[ob][:, 256 + jj * nq:256 + (jj + 1) * nq],
                    scalar1=EST[:, c:c + 1], scalar2=None, op0=ALU.add),
                    reads=[kb(ob), ("est",)], writes=[("rd",)])
            P.add("dve", lambda e: e.reciprocal(out=RD[:, 0:W], in_=RD[:, 0:W]), reads=[("rd",)], writes=[("rd",)])
        atk = [("at", 4 + 2 * kap, fq // 128), ("at", 5 + 2 * kap, fq // 128)]
        rd3 = RD[:, 0:W].rearrange("p (j q) -> p j q", j=2)
        P.add("dve", lambda e, kap=kap: e.tensor_tensor(
            out=rd3, in0=rd3, in1=AT[:, 4 + 2 * kap:6 + 2 * kap, fq + q0:fq + q0 + nq], op=ALU.mult),
            reads=[("rd",)] + atk, writes=[("rd",)])
        P.add("dve", lambda e, kap=kap, ob=ob: e.tensor_tensor(
            out=AT[:, 4 + 2 * kap:6 + 2 * kap, fq + q0:fq + q0 + nq],
            in0=PS[ob][:, 0:W].rearrange("p (j q) -> p j q", j=2), in1=rd3, op=ALU.mult),
            reads=[kb(ob), ("rd",)], writes=atk)

    LOOK = 2
    for it in items[:LOOK]:
        emit_S(it)
    for n, it in enumerate(items):
        emit_PV(it)
        if n + LOOK < len(items):
            emit_S(items[n + LOOK])

    if stop == 'attn':
        return finish()
    TMPS = [TMP, TMP2]

    def xring_load(i, xd):
        sl = i % 2
        P.add("sp", lambda e: e.dma_start(out=FA[:, sl * 1024:(sl + 1) * 1024], in_=xd),
              writes=[("fa", 2 * sl), ("fa", 2 * sl + 1)], dma=True)

    def outproj_prep(L, wd, blocks, slots=None, gts=None):
        if slots is None:
            s0 = load_w(wd, 0)
            s1 = load_w(wd, 512)
        else:
            s0, s1 = slots
        if gts is None:
            gts = {0: gate_tile(L, 0), 1: gate_tile(L, 1)}
        if L == 0:
            for i in range(2):
                xring_load(i, blocks[i][3])
        return (L, s0, s1, blocks, gts)

    def outproj_run(state, after_block=None):
        L, s0, s1, blocks, gts = state
        tctr = [0]
        for bi, (cond, yb, f0, xd, od) in enumerate(blocks):
            GTc, gtk = gts[cond]
            for nt, s in ((0, s0), (1, s1)):
                bk = banks(1)[0]
                for kc in range(8):
                    P.add("pe", lambda e, kc=kc, bk=bk, s=s, f0=f0: e.matmul(
                        PS[bk][:, :], lhsT=AT[:, kc, f0:f0 + 128], rhs=WR[:, s, kc, :], start=(kc == 0), stop=(kc == 7)),
                        reads=[("wr", s), ("at", kc, f0 // 128)], writes=[kb(bk)])
                if L == 0:
                    P.add("dve", lambda e, bk=bk, nt=nt, yb=yb, GTc=GTc: e.tensor_tensor(
                        out=YS(yb)[:, nt * 512:(nt + 1) * 512], in0=PS[bk][:, :], in1=GTc[:, nt * 512:(nt + 1) * 512],
                        op=ALU.mult),
                        reads=[kb(bk)] + gtk, writes=[("ys", yb)])
                else:
                    tm = TMPS[tctr[0] % 2]
                    tk = ("tmp", tctr[0] % 2)
                    tctr[0] += 1
                    P.add("dve", lambda e, bk=bk, nt=nt, tm=tm, GTc=GTc: e.tensor_tensor(
                        out=tm[:, :], in0=PS[bk][:, :], in1=GTc[:, nt * 512:(nt + 1) * 512], op=ALU.mult),
                        reads=[kb(bk)] + gtk, writes=[tk])
                    P.add("dve", lambda e, yb=yb, nt=nt, tm=tm: e.tensor_tensor(
                        out=YS(yb)[:, nt * 512:(nt + 1) * 512], in0=YS(yb)[:, nt * 512:(nt + 1) * 512], in1=tm[:, :],
                        op=ALU.add),
                        reads=[tk, ("ys", yb)], writes=[("ys", yb)])
            if L == 0:
                sl = bi % 2
                P.add("pool", lambda e, yb=yb, sl=sl: e.tensor_tensor(
                    out=YS(yb), in0=YS(yb), in1=FA[:, sl * 1024:(sl + 1) * 1024], op=ALU.add),
                    reads=[("ys", yb), ("fa", 2 * sl), ("fa", 2 * sl + 1)], writes=[("ys", yb)])
                if bi + 2 < len(blocks):
                    xring_load(bi + 2, blocks[bi + 2][3])
            if od is not None:
                P.add("sp", lambda e, yb=yb, od=od: e.dma_start(out=od, in_=YS(yb)), reads=[("ys", yb)], dma=True)
            if after_block is not None:
                after_block(bi)

    run_deferred(8)
    blocks0 = [(0, pb, pb * 128, xp_d[pb], None) for pb in range(4)]
    blocks0 += [(1, 4 + (wb - 1), 512 + (wb - 1) * 128, xs_d[wb], None) for wb in range(1, 11)]
    st0 = outproj_prep(0, wout_e_d, blocks0, gts=gts_l0)
    P.barrier()

    def ysrc(b):
        return YS(b), [("ys", b)]
    g1 = {3: (0, [((lambda pb=pb: ysrc(pb)), pb) for pb in range(4)])}
    for lo, hi in ((1, 5), (5, 9), (9, 11)):
        g1[4 + hi - 2] = (1, [((lambda wb=wb: ysrc(4 + wb - 1)), 4 + wb) for wb in range(lo, hi)])

    pend = [None]

    def after_l0_block(bi):
        if bi in g1:
            if pend[0] is not None:
                phase1_b(pend[0])
            pend[0] = phase1_a(1, g1[bi])
    outproj_run(st0, after_block=after_l0_block)
    phase1_b(pend[0])
    zero_invalid((1, 10))

    if stop == 'out0':
        return finish()
    U = FA[:, 0:UW]
    PA = FB[:, 0:UW]
    PB = FC[:, 0:UW]
    P.add("dve", lambda e: e.memset(FA[:, :], 0.0), writes=fa_all)
    fc_all = [("fc", i) for i in range(4)]

    def useg(buf, ti):
        if ti == 0:
            return buf[:, 16:592].rearrange("p (s c) -> p s c", c=288)[:, :, 0:256]
        f0 = TILES[ti][2] - 512
        n = TILES[ti][1]
        return buf[:, 592 + f0:592 + f0 + n]

    wslot1 = {}

    def l1_pool(g, cc):
        s = wslot1[g]
        bks = banks(4)
        proj(s, cc * 128, TILES, bks, tile_outer=(g == 0 and cc == 0))
        return bks

    def l1_pool_post(g, cc, bks):
        w = (2, 4, 8, 16)[g]
        if True:
            for ti in range(4):
                P.add("act", lambda e, ti=ti, bk=bks[ti]: e.activation(out=useg(U, ti), in_=pseg(bk, ti), func=AF.Copy),
                      reads=[kb(bks[ti])], writes=fa_all)
            lo, hi = 1, UW - 1
            P.add("dve", lambda e, lo=lo, hi=hi: e.tensor_tensor(out=PA[:, lo:hi], in0=U[:, lo:hi], in1=U[:, lo - 1:hi - 1],
                                                                 op=ALU.add),
                  reads=fa_all, writes=fb_all)
            cur, curk, oth, othk = PA, fb_all, PB, fc_all
            for sh in (1, 2, 4)[:g]:
                lo, hi = lo + sh, hi - sh
                P.add("dve", lambda e, lo=lo, hi=hi, sh=sh, cur=cur, oth=oth: e.tensor_tensor(
                    out=oth[:, lo:hi], in0=cur[:, lo + sh:hi + sh], in1=cur[:, lo - sh:hi - sh], op=ALU.add),
                    reads=curk, writes=othk)
                cur, curk, oth, othk = oth, othk, cur, curk
            for ti, (h0, n, f0) in enumerate(TILES):
                if ti == 0:
                    o = YT[:, cc, 0:512].rearrange("p (s c) -> p s c", c=256)
                else:
                    o = YT[:, cc, f0:f0 + n]
                P.add("dve", lambda e, ti=ti, o=o, cur=cur, w=w: e.scalar_tensor_tensor(
                    out=o, in0=useg(cur, ti), scalar=1.0 / w, in1=useg(U, ti), op0=ALU.mult, op1=ALU.subtract),
                    reads=curk + fa_all, writes=fkeys("yt", cc, f0, n))
            for sq in range(2):
                for ed in range(2):
                    c0 = 16 + sq * 288 + (0 if ed == 0 else 248)
                    fcol = sq * 256 + (0 if ed == 0 else 248)
                    tcol = C_PCE + g * 16 + ed * 8
                    P.add("pool", lambda e, c0=c0, tcol=tcol, cur=cur: e.tensor_tensor(
                        out=TMP3[:, 0:8], in0=cur[:, c0:c0 + 8], in1=SM[:, tcol:tcol + 8], op=ALU.mult),
                        reads=curk + [("sm",)], writes=[("tmp3",)])
                    P.add("pool", lambda e, c0=c0, fcol=fcol, cc=cc: e.tensor_tensor(
                        out=YT[:, cc, fcol:fcol + 8], in0=TMP3[:, 0:8], in1=U[:, c0:c0 + 8], op=ALU.subtract),
                        reads=[("tmp3",)] + fa_all, writes=fkeys("yt", cc, (fcol // 128) * 128, 128))
            for ed in range(2):
                sc = 128 if ed == 0 else 1144
                c0 = 592 + sc
                fcol = 512 + sc
                tcol = C_SCE + g * 16 + ed * 8
                P.add("pool", lambda e, c0=c0, tcol=tcol, cur=cur: e.tensor_tensor(
                    out=TMP3[:, 8:16], in0=cur[:, c0:c0 + 8], in1=SM[:, tcol:tcol + 8], op=ALU.mult),
                    reads=curk + [("sm",)], writes=[("tmp3b",)])
                P.add("pool", lambda e, c0=c0, fcol=fcol, cc=cc: e.tensor_tensor(
                    out=YT[:, cc, fcol:fcol + 8], in0=TMP3[:, 8:16], in1=U[:, c0:c0 + 8], op=ALU.subtract),
                    reads=[("tmp3b",)] + fa_all, writes=fkeys("yt", cc, (fcol // 128) * 128, 128))

    def l1_projg(g, cc):
        s = wslot1[g]
        bks = banks(3)
        proj(s, 256 + cc * 128, TILES_G, bks)
        for ti, (h0, n, f0) in enumerate(TILES_G):
            P.add("act", lambda e, bk=bks[ti], n=n, f0=f0, cc=cc: e.activation(
                out=SGT[:, cc, f0:f0 + n], in_=PS[bk][:, 0:n], func=AF.Silu),
                reads=[kb(bks[ti])], writes=fkeys("sgt", cc, f0, n))

    def l1_poolw(g, avoid=()):
        free = [b for b in range(8) if b not in avoid]
        pc = [0]
        for dd in range(2):
            for ti, (h0, n, f0) in enumerate(TILES_G):
                bk = free[pc[0] % len(free)]
                pc[0] += 1
                for cc in range(2):
                    P.add("pe", lambda e, bk=bk, cc=cc, dd=dd, n=n, f0=f0, g=g: e.matmul(
                        PS[bk][:, 0:n], lhsT=PW[:, g, cc, dd * 128:(dd + 1) * 128], rhs=YT[:, cc, f0:f0 + n],
                        start=(cc == 0), stop=(cc == 1)),
                        reads=[("pw",)] + fkeys("yt", cc, f0, n), writes=[kb(bk)])
                ch = 2 * g + dd
                P.add("dve", lambda e, bk=bk, n=n, f0=f0, dd=dd, ch=ch: e.scalar_tensor_tensor(
                    out=AT[:, ch, f0:f0 + n], in0=PS[bk][:, 0:n], scalar=SM[:, C_PSC + ch:C_PSC + ch + 1],
                    in1=SGT[:, dd, f0:f0 + n], op0=ALU.mult, op1=ALU.mult),
                    reads=[kb(bk), ("sm",)] + fkeys("sgt", dd, f0, n), writes=fkeys("at", ch, f0, n))

    for g in range(3):
        wslot1[g] = load_w(win_o_d, g * 512)
    for g in range(4):
        if g == 1:
            wslot1[3] = load_w(win_o_d, 3 * 512)
        elif g == 2:
            wslot1["o0"] = load_w(wout_o_d, 0)
        elif g == 3:
            wslot1["o1"] = load_w(wout_o_d, 512)
        bks0 = l1_pool(g, 0)
        if g > 0:
            l1_poolw(g - 1, avoid=bks0)
        l1_pool_post(g, 0, bks0)
        bks1 = l1_pool(g, 1)
        l1_pool_post(g, 1, bks1)
        l1_projg(g, 0)
        l1_projg(g, 1)
    l1_poolw(3)

    blocks1 = [(0, pb, pb * 128, None, yp_d[pb]) for pb in range(4)]
    blocks1 += [(1, 4 + (wb - 1), 512 + (wb - 1) * 128, None, ysm_d[wb - 2]) for wb in range(2, 10)]
    outproj_run(outproj_prep(1, wout_o_d, blocks1, slots=(wslot1["o0"], wslot1["o1"])))

    return finish()


_NC_CACHE = {}


def _host_tables(side):
    a = side * 1024
    j = np.arange(1536)
    t = a - 256 + j
    tv = np.clip(t, 0, 2047)
    row = (tv // 64).astype(np.float64)
    col = (tv % 64).astype(np.float64)
    inv = 10000.0 ** (-np.arange(16, dtype=np.float64) / 16)
    p = np.arange(128)
    d = p % 64
    f = d % 16
    pos = np.where((d < 32)[:, None], row[None, :], col[None, :])
    ang = pos * inv[f][:, None]
    tabc = np.cos(ang).astype(np.float32)
    sign = np.where((d % 32) < 16, -1.0, 1.0)
    tabs = (np.sin(ang) * sign[:, None]).astype(np.float32)
    return tabc, tabs


def _consts():
    cb = np.zeros((128, NCB), np.float32)
    cb[:, CB_ID:CB_ID + 128] = np.eye(128, dtype=np.float32)
    m = np.arange(128)
    partner = np.where((m % 32) < 16, m + 16, m - 16)
    pm = np.zeros((128, 128), np.float32)
    pm[partner, m] = 1.0
    cb[:, CB_PM:CB_PM + 128] = pm
    cb[:, CB_BO:CB_BO + 128] = (m[:, None] // 64 == m[None, :] // 64).astype(np.float32)
    kk = m[:, None]
    qq = m[None, :]
    for rep in range(2):
        cb[:, CB_NML + rep * 128:CB_NML + (rep + 1) * 128] = np.where(qq <= kk, 0.0, NEGM)
        cb[:, CB_NMR + rep * 128:CB_NMR + (rep + 1) * 128] = np.where(kk <= qq, 0.0, NEGM)
    cb[:, CB_O3:CB_O3 + 64] = 1.0
    cb[:, CB_O3 + 128:CB_O3 + 192] = 1.0
    return cb


def make_in_maps(x_prompt, x_sample, cache_k, cache_v, c, c_ctx,
                 ada_w_e, ada_b_e, norm_g_e, w_in_e, conv_w, conv_b, q_norm_g, k_norm_g, sink, w_out_e,
                 ada_w_o, ada_b_o, norm_g_o, w_in_o, pool_w, pool_scale, w_out_o):
    f = lambda a: np.ascontiguousarray(np.asarray(a, dtype=np.float32))
    x_prompt, x_sample, cache_k, cache_v, c, c_ctx = map(f, (x_prompt, x_sample, cache_k, cache_v, c, c_ctx))
    ada_w_e, ada_b_e, norm_g_e, w_in_e, conv_w, conv_b = map(f, (ada_w_e, ada_b_e, norm_g_e, w_in_e, conv_w, conv_b))
    q_norm_g, k_norm_g, sink, w_out_e = map(f, (q_norm_g, k_norm_g, sink, w_out_e))
    ada_w_o, ada_b_o, norm_g_o, w_in_o, pool_w, pool_scale, w_out_o = map(
        f, (ada_w_o, ada_b_o, norm_g_o, w_in_o, pool_w, pool_scale, w_out_o))

    W = w_in_e[0]
    bg, cg, xs_, ga, q_, k_, v_, gb = (W[:, 0:512], W[:, 512:1024], W[:, 1024:1536], W[:, 1536:2048],
                                       W[:, 2048:2560], W[:, 2560:2688], W[:, 2688:2816], W[:, 2816:3328])
    groups = []
    for j in range(4):
        sl = slice(j * 128, (j + 1) * 128)
        groups += [bg[:, sl], cg[:, sl], xs_[:, sl], ga[:, sl]]
    groups += [q_, gb, k_, v_, k_[:, 0:64], k_[:, 0:64], k_[:, 64:128], k_[:, 64:128]]
    win_e = np.ascontiguousarray(np.concatenate(groups, axis=1))
    Wo = w_in_o[0]
    go = []
    for g in range(4):
        go += [Wo[:, g * 256:(g + 1) * 256], Wo[:, 1024 + g * 256:1024 + (g + 1) * 256]]
    win_o = np.ascontiguousarray(np.concatenate(go, axis=1))
    cb = _consts()
    idf = np.eye(128, dtype=np.float32)
    p = np.arange(128)

    def fm(v, n):
        return v.reshape(n, 128).T

    in_maps = []
    for r in range(8):
        b, side = r // 2, r % 2
        a = side * 1024
        xp = x_prompt[2 * r:2 * r + 2].reshape(4, 128, D)
        xs = np.zeros((12 * 128, D), np.float32)
        t = a - 256 + np.arange(1536)
        ok = (t >= 0) & (t < 2048)
        xs[ok] = x_sample[b][t[ok]]
        vflag = ok.reshape(12, 128)[:, 0].astype(np.float32)
        smalls = np.zeros((128, NSMALL), np.float32)
        cond = np.stack([fm(c_ctx, 8), fm(c[b], 8)], axis=2)
        smalls[:, C_COND:C_COND + 16] = cond.reshape(128, 16)
        smalls[:, C_ABE:C_ABE + 24] = fm(ada_b_e[0], 24)
        smalls[:, C_ABO:C_ABO + 24] = fm(ada_b_o[0], 24)
        smalls[:, C_NGE:C_NGE + 8] = fm(norm_g_e[0], 8)
        smalls[:, C_NGO:C_NGO + 8] = fm(norm_g_o[0], 8)
        for k in range(3):
            smalls[:, C_CW + k * 4:C_CW + k * 4 + 4] = fm(conv_w[0, k], 4)
        smalls[:, C_CB:C_CB + 4] = fm(conv_b[0], 4)
        smalls[:, C_GQ] = q_norm_g[0][p % 64]
        smalls[:, C_GK] = k_norm_g[0][p % 64]
        for cc in range(4):
            smalls[:, C_SINK + cc] = np.where(p >= 64, sink[0, 2 * cc + 1], sink[0, 2 * cc])
        smalls[:, C_PSC:C_PSC + 8] = fm(pool_scale[0], 8)
        smalls[:, C_VF:C_VF + 12] = vflag[None, :]
        smalls[:, 127] = EPS
        for g, w in enumerate((2, 4, 8, 16)):
            L = 2048
            tt = np.arange(L)
            lo = np.clip(tt - w // 2, 0, L)
            hi = np.clip(tt + w // 2, 0, L)
            rc = (1.0 / (hi - lo)).astype(np.float32)
            smalls[:, C_PCE + g * 16:C_PCE + g * 16 + 8] = rc[None, 0:8]
            smalls[:, C_PCE + g * 16 + 8:C_PCE + g * 16 + 16] = rc[None, L - 8:L]
            smalls[:, C_SCE + g * 16:C_SCE + g * 16 + 8] = rc[None, 0:8] if side == 0 else 1.0 / w
            smalls[:, C_SCE + g * 16 + 8:C_SCE + g * 16 + 16] = rc[None, L - 8:L] if side == 1 else 1.0 / w
        tabc, tabs = _host_tables(side)
        ck = cache_k[b, 0]
        kc = np.ascontiguousarray(np.transpose(ck, (2, 1, 0)))
        kc = np.concatenate([kc, kc], axis=0)
        cv = np.ascontiguousarray(cache_v[b, 0].reshape(2, 128, 128).transpose(1, 0, 2))
        in_maps.append(dict(
            xp=np.ascontiguousarray(xp), xs=xs.reshape(12, 128, D), smalls=smalls,
            gkrow=np.ascontiguousarray(np.broadcast_to(k_norm_g[0][None, :], (128, 64))),
            tabc=tabc, tabs=tabs, cb=cb, idf=idf, kc=np.ascontiguousarray(kc), cv=cv,
            wada_e=ada_w_e[0], wada_o=ada_w_o[0], win_e=win_e, wout_e=w_out_e[0], win_o=win_o,
            wout_o=w_out_o[0], pw=pool_w[0]))

    return in_maps


def kernel(**inputs):
    in_maps = make_in_maps(**inputs)
    if "nc" not in _NC_CACHE:
        _NC_CACHE["nc"] = build_program()
    nc = _NC_CACHE["nc"]
    res = run_bass_kernel_spmd(nc, in_maps, core_ids=list(range(8)))
    yp = np.zeros((16, 256, D), np.float32)
    ys = np.zeros((4, 2048, D), np.float32)
    nk = np.zeros((16, 1, 256, 2, 64), np.float32)
    nv = np.zeros((16, 1, 256, 2, 64), np.float32)
    for r in range(8):
        o = res.results[r]
        b, side = r // 2, r % 2
        yp[2 * r:2 * r + 2] = np.asarray(o["yp"]).reshape(2, 256, D)
        ys[b, side * 1024:(side + 1) * 1024] = np.asarray(o["ysm"]).reshape(1024, D)
        nk[2 * r:2 * r + 2, 0] = np.asarray(o["nk"]).reshape(2, 256, 2, 64)
        nv[2 * r:2 * r + 2, 0] = np.asarray(o["nv"]).reshape(2, 256, 2, 64)
    return yp, ys, nk, nv
```

```python
import os
import numpy as np
from contextlib import ExitStack
import concourse.bass as bass
import concourse.mybir as mybir
from concourse.bass_utils import run_bass_kernel_spmd

F32 = mybir.dt.float32
BF16 = mybir.dt.bfloat16
AF = mybir.ActivationFunctionType
ALU = mybir.AluOpType

D = 1024
EPS = 1e-6
NEGM = -30000.0
NDS = 16


class Prog:
    ENGS = ("pe", "act", "dve", "pool", "sp")

    def __init__(self):
        self.ops = []
        self.lastw = {}
        self.readers = {}
        self.bar_deps = set()
        self.bar_pending = set()

    def add(self, eng, fn, reads=(), writes=(), dma=False):
        i = len(self.ops)
        deps = set()
        for k in reads:
            w = self.lastw.get(k)
            if w is not None:
                deps.add(w)
            if k[0] == "ps":
                for rd in self.readers.get(k, ()):
                    if self.ops[rd]["eng"] != eng:
                        deps.add(rd)
        best = {}
        for k in writes:
            w = self.lastw.get(k)
            if w is not None:
                deps.add(w)
                assert not (k[0] == "wr" and not self.readers.get(k)), "dead weight load into %r" % (k,)
            for rd in self.readers.get(k, ()):
                o = self.ops[rd]
                if o["dma"]:
                    deps.add(rd)
                elif rd > best.get(o["eng"], -1):
                    best[o["eng"]] = rd
        deps |= set(best.values())
        for k in reads:
            self.readers.setdefault(k, set()).add(i)
        for k in writes:
            self.lastw[k] = i
            self.readers[k] = set()
        if eng in self.bar_pending:
            deps |= self.bar_deps
            self.bar_pending.discard(eng)
        deps.discard(i)
        self.ops.append(dict(eng=eng, fn=fn, deps=deps, dma=dma))
        return i

    def barrier(self):
        last = {}
        b = set()
        for i, o in enumerate(self.ops):
            if o["dma"]:
                b.add(i)
            else:
                last[o["eng"]] = i
        b |= set(last.values())
        self.bar_deps = b
        self.bar_pending = set(self.ENGS)

    def emit(self, nc, es):
        ops = self.ops
        esem = {e: es.enter_context(nc.semaphore("sem_" + e)) for e in self.ENGS}
        dsem = [es.enter_context(nc.semaphore("dsem%d" % i)) for i in range(NDS)]
        dcount = [0] * NDS
        dlast = [None] * NDS
        nd = {"sp": 0, "pool": 0}
        half = NDS // 2
        for i, o in enumerate(ops):
            if o["dma"]:
                q = o["eng"]
                s = (nd[q] % half) + (0 if q == "sp" else half)
                nd[q] += 1
                if dlast[s] is not None:
                    o["deps"].add(dlast[s])
                dcount[s] += 1
                o["dsem"] = s
                o["dval"] = 16 * dcount[s]
                dlast[s] = i
        sig = set()
        for i, o in enumerate(ops):
            for d in o["deps"]:
                p = ops[d]
                if p["dma"]:
                    continue
                if p["eng"] == "pe" and o["eng"] == "pe" and not o["dma"]:
                    continue
                sig.add(d)
        cnt = {e: 0 for e in self.ENGS}
        for i, o in enumerate(ops):
            if (not o["dma"]) and i in sig:
                cnt[o["eng"]] += 1
                o["sval"] = cnt[o["eng"]]
        final_d = list(dcount)
        block = es.enter_context(nc.Block())

        def run(eng_name, e):
            waited = {}
            for i, o in enumerate(ops):
                if o["eng"] != eng_name:
                    continue
                need = {}
                for d in o["deps"]:
                    p = ops[d]
                    if p["dma"]:
                        ch = ("d", p["dsem"])
                        v = p["dval"]
                    else:
                        if p["eng"] == "pe" and eng_name == "pe" and not o["dma"]:
                            continue
                        ch = ("e", p["eng"])
                        v = p["sval"]
                    if v > need.get(ch, 0):
                        need[ch] = v
                for ch, v in need.items():
                    if waited.get(ch, 0) >= v:
                        continue
                    waited[ch] = v
                    s = dsem[ch[1]] if ch[0] == "d" else esem[ch[1]]
                    e.wait_ge(s, v)
                ins = o["fn"](e)
                if o["dma"]:
                    ins.then_inc(dsem[o["dsem"]], 16)
                elif i in sig:
                    ins.then_inc(esem[eng_name], 1)
            if eng_name == "sp":
                for s in range(NDS):
                    if final_d[s] > 0:
                        e.wait_ge(dsem[s], 16 * final_d[s])

        @block.sync
        def _(e):
            run("sp", e)

        @block.gpsimd
        def _(e):
            run("pool", e)

        @block.scalar
        def _(e):
            run("act", e)

        @block.vector
        def _(e):
            run("dve", e)

        @block.tensor
        def _(e):
            run("pe", e)


HALO = 16
TILES = [(0, 512, 0), (512 + 128 + (128 - HALO), 384 + HALO, 512 + (128 - HALO)), (512 + 640, 512, 1024),
         (512 + 1152, 128 + HALO, 1536)]
NFULL = 1792
TILES_G = [(0, 512, 0), (512 + 256, 512, 512 + 128), (512 + 768, 512, 512 + 640)]
CXW = 1796
UW = 1888

C_COND, C_ABE, C_ABO, C_NGE, C_NGO, C_CW, C_CB, C_GQ, C_GK, C_SINK, C_PSC, C_VF, C_PCE, C_SCE = (
    0, 16, 40, 64, 72, 80, 92, 96, 97, 98, 102, 110, 128, 256)
NSMALL = 320
CB_ID, CB_PM, CB_BO, CB_NML, CB_NMR, CB_O3 = 0, 128, 256, 384, 640, 896
NCB = 1088


def build_program(stop=None):
    nc = bass.Bass("TRN2", target_bir_lowering=False)
    P = Prog()
    es = ExitStack()

    def din(name, shape):
        return nc.dram_tensor(name, list(shape), F32, kind="ExternalInput").ap()

    def dout(name, shape):
        return nc.dram_tensor(name, list(shape), F32, kind="ExternalOutput").ap()

    xp_d = din("xp", [4, 128, D])
    xs_d = din("xs", [12, 128, D])
    smalls_d = din("smalls", [128, NSMALL])
    gkrow_d = din("gkrow", [128, 64])
    tabc_d = din("tabc", [128, 1536])
    tabs_d = din("tabs", [128, 1536])
    cb_d = din("cb", [128, NCB])
    idf_d = din("idf", [128, 128])
    kc_d = din("kc", [128, 2, 256])
    cv_d = din("cv", [128, 2, 128])
    wada_e_d = din("wada_e", [D, 3072]).rearrange("(kc p) n -> p kc n", p=128)
    wada_o_d = din("wada_o", [D, 3072]).rearrange("(kc p) n -> p kc n", p=128)
    win_e_d = din("win_e", [D, 3584]).rearrange("(kc p) n -> p kc n", p=128)
    wout_e_d = din("wout_e", [D, D]).rearrange("(kc p) n -> p kc n", p=128)
    win_o_d = din("win_o", [D, 2048]).rearrange("(kc p) n -> p kc n", p=128)
    wout_o_d = din("wout_o", [D, D]).rearrange("(kc p) n -> p kc n", p=128)
    pw_d = din("pw", [4, 256, 256]).rearrange("g (cc p) n -> p g cc n", p=128)
    yp_d = dout("yp", [4, 128, D])
    ysm_d = dout("ysm", [8, 128, D])
    nk_d = dout("nk", [4, 128, 128])
    nv_d = dout("nv", [4, 128, 128])

    def finish():
        P.emit(nc, es)
        es.close()
        return nc

    def sb(name, shape, dt):
        return es.enter_context(nc.sbuf_tensor(name, list(shape), dt))

    BIG = sb("big", [128, 14336], F32)
    HT = sb("ht", [128, 8, 2048], BF16)
    AT = sb("at", [128, 8, NFULL], BF16)
    WR = sb("wr", [128, 3, 8, 512], BF16)
    FA = sb("fa", [128, 2048], F32)
    FB = sb("fb", [128, 2048], F32)
    FC = sb("fc", [128, UW], F32)
    GT = sb("gt", [128, D], F32)
    YT = sb("yt", [128, 2, NFULL], BF16)
    SGT = sb("sgt", [128, 2, NFULL], BF16)
    QG = sb("qg", [128, 512], BF16)
    SQ = sb("sq", [128, 512], BF16)
    QG2 = sb("qg2", [128, 512], BF16)
    SQ2 = sb("sq2", [128, 512], BF16)
    JUNK = sb("junk", [128, 1024], BF16)
    SM = sb("sm", [128, NSMALL], F32)
    CB = sb("cbs", [128, NCB], BF16)
    IDF = sb("idfs", [128, 128], F32)
    GKR = sb("gkr", [128, 64], F32)
    PW = sb("pws", [128, 4, 2, 256], BF16)
    ST = sb("st", [128, 16], BF16)
    MA = sb("ma", [128, 2, 48], F32)
    GS = sb("gs", [128, 2, 2, 8], F32)
    SS = sb("ssx", [128, 64], F32)
    RK = sb("rk", [128, 16, 2], F32)
    RKS = sb("rks", [128, 16, 2], F32)
    EST = sb("est", [128, 4], F32)
    VALX = sb("valx", [128, 4, 192], BF16)
    GBT = sb("gbt", [128, 128], F32)
    KO = FA[:, 0:512].rearrange("p (b c) -> p b c", b=4)
    VO = FA[:, 512:1024].rearrange("p (b c) -> p b c", b=4)
    RD = sb("rd", [128, 256], F32)
    TMP = sb("tmp", [128, 512], F32)
    TMP2 = sb("tmp2", [128, 512], F32)
    TMP3 = sb("tmp3", [128, 16], F32)
    PS = [es.enter_context(nc.psum_tensor("ps%d" % i, [128, 512], F32)) for i in range(8)]
    QGS_K = [(QG, ("qg",)), (QG2, ("qg2",))]

    def bigv(lo, n32):
        return BIG[:, lo:lo + n32]
    KK = bigv(0, 2048).bitcast(BF16).rearrange("p (k t) -> p k t", k=2)
    VV = bigv(2048, 3456).bitcast(BF16).rearrange("p (b k c) -> p b k c", b=18, k=2)
    QT = bigv(5504, 3584).bitcast(BF16).rearrange("p (c t) -> p c t", c=4)
    TABC = bigv(9088, 1536)
    TABS = bigv(10624, 1536)
    ET = bigv(12160, 1024).bitcast(BF16).rearrange("p (s t) -> p s t", s=4)
    KC = bigv(13184, 256).bitcast(BF16).rearrange("p (k t) -> p k t", k=2)

    def YS(b):
        return BIG[:, b * 1024:(b + 1) * 1024]
    XN = FB[:, :].bitcast(BF16).rearrange("p (s f) -> p s f", s=4)

    bank_ctr = [0]

    def banks(n=1):
        r = [(bank_ctr[0] + i) % 8 for i in range(n)]
        bank_ctr[0] = (bank_ctr[0] + n) % 8
        return r

    def kb(b):
        return ("ps", b)

    def sm(c0, c1=None):
        return SM[:, c0:(c0 + 1 if c1 is None else c1)]

    P.add("sp", lambda e: e.dma_start(out=SM[:, :], in_=smalls_d), writes=[("sm",)], dma=True)
    P.add("sp", lambda e: e.dma_start(out=IDF[:, :], in_=idf_d), writes=[("idf",)], dma=True)
    P.add("sp", lambda e: e.dma_start(out=GKR[:, :], in_=gkrow_d), writes=[("gkr",)], dma=True)
    P.add("sp", lambda e: e.dma_start(out=TABC, in_=tabc_d), writes=[("tab",)], dma=True)
    P.add("sp", lambda e: e.dma_start(out=TABS, in_=tabs_d), writes=[("tab",)], dma=True)
    P.add("pool", lambda e: e.dma_start(out=CB[:, :], in_=cb_d), writes=[("cb",)], dma=True)
    P.add("dve", lambda e: e.memset(VV[:, :, :, 64:128], 0.0), writes=[("vv", b) for b in range(18)])
    P.add("dve", lambda e: e.memset(SS[:, :], 0.0), writes=[("ss", i) for i in range(64)])
    P.add("dve", lambda e: e.memset(RK[:, :, :], 0.0), writes=[("rk", i) for i in range(16)])
    P.add("act", lambda e: e.activation(out=ST[:, :], in_=SM[:, C_COND:C_COND + 16], func=AF.Silu),
          reads=[("sm",)], writes=[("st",)])
    for i, wb in enumerate((0, 1, 10, 11)):
        P.add("dve", lambda e, i=i, wb=wb: e.tensor_scalar(out=VALX[:, i, :], in0=CB[:, CB_O3:CB_O3 + 192],
                                                            scalar1=SM[:, C_VF + wb:C_VF + wb + 1], scalar2=None,
                                                            op0=ALU.mult),
              reads=[("sm",), ("cb",)], writes=[("valx", i)])
    P.add("act", lambda e: e.activation(out=EST[:, :], in_=SM[:, C_SINK:C_SINK + 4], func=AF.Exp),
          reads=[("sm",)], writes=[("est",)])

    wr_ctr = [0]

    def load_w(dram, c0, ncols=512, slot=None):
        if slot is None:
            s = wr_ctr[0] % 3
            wr_ctr[0] += 1
        else:
            s = slot
        P.add("pool", lambda e: e.dma_start(out=WR[:, s, :, 0:ncols], in_=dram[:, :, c0:c0 + ncols]),
              writes=[("wr", s)], dma=True)
        return s

    def adaln_group(L, g, slot=None, bank=None):
        wd, cbias = ((wada_e_d, C_ABE), (wada_o_d, C_ABO))[L]
        bk = banks(1)[0] if bank is None else bank
        s = load_w(wd, g * 512, slot=slot)
        for oc4 in range(4):
            for kc in range(8):
                P.add("pe", lambda e, s=s, oc4=oc4, kc=kc, bk=bk: e.matmul(
                    PS[bk][:, oc4 * 2:oc4 * 2 + 2], lhsT=WR[:, s, kc, oc4 * 128:(oc4 + 1) * 128],
                    rhs=ST[:, kc * 2:kc * 2 + 2], start=(kc == 0), stop=(kc == 7)),
                    reads=[("wr", s), ("st",)], writes=[kb(bk)])
        third = (g * 4) // 8
        for c in range(2):
            P.add("dve", lambda e, L=L, bk=bk, cbias=cbias, c=c, g=g: e.tensor_tensor(
                out=MA[:, L, g * 8:(g + 1) * 8].rearrange("p (o c) -> p o c", c=2)[:, :, c],
                in0=PS[bk][:, 0:8].rearrange("p (o c) -> p o c", c=2)[:, :, c],
                in1=SM[:, cbias + g * 4:cbias + g * 4 + 4], op=ALU.add),
                reads=[kb(bk), ("sm",)], writes=[("ma", L, third, g % 2)])
        if g == 3:
            cng = C_NGE if L == 0 else C_NGO
            for c in range(2):
                P.add("dve", lambda e, L=L, c=c, cng=cng: e.scalar_tensor_tensor(
                    out=GS[:, L, c, :], in0=MA[:, L, :].rearrange("p (o c) -> p o c", c=2)[:, 8:16, c], scalar=1.0,
                    in1=SM[:, cng:cng + 8], op0=ALU.add, op1=ALU.mult),
                    reads=[("ma", L, 1, 0), ("ma", L, 1, 1), ("sm",)], writes=[("gs", L, c)])

    for g in range(4):
        adaln_group(0, g)
    P.add("pool", lambda e: e.dma_start(out=PW[:, :, :, :], in_=pw_d), writes=[("pw",)], dma=True)
    P.add("pool", lambda e: e.dma_start(out=KC, in_=kc_d), writes=[("kc",)], dma=True)
    for half in (0, 128):
        P.add("pool", lambda e, half=half: e.dma_start(out=VV[:, 16:18, :, half:half + 64],
                                                         in_=cv_d.rearrange("p b (k d) -> p b k d", k=2)),
              writes=[("vv", 16), ("vv", 17)], dma=True)
    deferred = [(0, 4), (0, 5)] + [(1, g) for g in range(6)]

    def run_deferred(n=1, slot=None, bank=None):
        for _ in range(n):
            if deferred:
                adaln_group(*deferred.pop(0), slot=slot, bank=bank)

    def shiftv(L, c, kc):
        return MA[:, L, kc * 2 + c:kc * 2 + c + 1]

    def gatev(L, c, kc):
        return MA[:, L, (16 + kc) * 2 + c:(16 + kc) * 2 + c + 1]

    if stop == 'adaln':
        return finish()
    tog = [0]

    def phase1(L, groups):
        for grp in groups:
            phase1_b(phase1_a(L, grp))

    def phase1_a(L, grp):
        cond, blks = grp
        if True:
            for j, (getter, hb) in enumerate(blks):
                src, skeys = getter()
                col = (tog[0] % 32)
                tog[0] += 1
                P.add("act", lambda e, src=src, col=col: e.activation(out=JUNK[:, :], in_=src, func=AF.Square,
                                                                      accum_out=SS[:, col:col + 1]),
                      reads=skeys, writes=[("junk",), ("ss", col)])
                P.add("act", lambda e, col=col: e.activation(out=SS[:, 32 + col:33 + col], in_=SS[:, col:col + 1],
                                                              func=AF.Sqrt, scale=1.0 / D, bias=SM[:, 127:128]),
                      reads=[("ss", col), ("sm",)], writes=[("ss", 32 + col)])
                P.add("dve", lambda e, col=col: e.reciprocal(out=SS[:, 32 + col:33 + col], in_=SS[:, 32 + col:33 + col]),
                      reads=[("ss", 32 + col)], writes=[("ss", 32 + col)])
                P.add("dve", lambda e, src=src, col=col, j=j: e.tensor_scalar(
                    out=XN[:, j, :], in0=src, scalar1=SS[:, 32 + col:33 + col], scalar2=None, op0=ALU.mult),
                    reads=list(skeys) + [("ss", 32 + col)], writes=[("fb", j)])
        return (L, cond, blks)

    def phase1_b(state):
        L, cond, blks = state
        if True:
            bks = banks(4)
            for j, (getter, hb) in enumerate(blks):
                for kc in range(8):
                    bk = bks[kc // 2]
                    off = (kc % 2) * 256 + j * 64
                    P.add("pe", lambda e, bk=bk, off=off, j=j, kc=kc: e.transpose(
                        PS[bk][:, off:off + 64].bitcast(BF16), XN[:, j, kc * 128:(kc + 1) * 128],
                        CB[:, CB_ID:CB_ID + 128]),
                        reads=[("fb", j), ("cb",)], writes=[kb(bk)])
            n = len(blks)
            hb0 = blks[0][1]
            for kc in range(8):
                bk = bks[kc // 2]
                src = PS[bk][:, (kc % 2) * 256:(kc % 2) * 256 + n * 64].bitcast(BF16)
                dst = HT[:, kc, hb0 * 128:(hb0 + n) * 128]
                wk = [("ht", hb0 + i) for i in range(n)]
                if kc % 4 == 0:
                    P.add("act", lambda e, src=src, dst=dst, kc=kc, cond=cond: e.activation(
                        out=dst, in_=src, func=AF.Identity, scale=GS[:, L, cond, kc:kc + 1], bias=shiftv(L, cond, kc)),
                        reads=[kb(bk), ("gs", L, cond), ("ma", L, 0, 0), ("ma", L, 0, 1)], writes=wk)
                else:
                    P.add("dve", lambda e, src=src, dst=dst, kc=kc, cond=cond: e.tensor_scalar(
                        out=dst, in0=src, scalar1=GS[:, L, cond, kc:kc + 1], scalar2=shiftv(L, cond, kc),
                        op0=ALU.mult, op1=ALU.add),
                        reads=[kb(bk), ("gs", L, cond), ("ma", L, 0, 0), ("ma", L, 0, 1)], writes=wk)

    def zero_invalid(wbs):
        for wb in wbs:
            hb = 4 + wb
            P.add("dve", lambda e, wb=wb, hb=hb: e.tensor_scalar(
                out=HT[:, :, hb * 128:(hb + 1) * 128], in0=HT[:, :, hb * 128:(hb + 1) * 128],
                scalar1=SM[:, C_VF + wb:C_VF + wb + 1], scalar2=None, op0=ALU.mult),
                reads=[("ht", hb), ("sm",)], writes=[("ht", hb)])

    xr = [0]

    ATX = AT[:, :, :].rearrange("p a b -> p (a b)").bitcast(F32)
    NXS = 4

    def atx_keys(sl):
        ks = set()
        for col in range(sl * 2048, (sl + 1) * 2048, 128):
            ks.add(("at", col // NFULL, (col % NFULL) // 128))
            ks.add(("at", (col + 127) // NFULL, ((col + 127) % NFULL) // 128))
        return sorted(ks)

    def xsrc(dram_blk):
        s = xr[0] % NXS
        xr[0] += 1
        keys = atx_keys(s)
        P.add("sp", lambda e: e.dma_start(out=ATX[:, s * 1024:(s + 1) * 1024], in_=dram_blk),
              writes=keys, dma=True)
        return ATX[:, s * 1024:(s + 1) * 1024], keys

    for cond, specs in ((0, [("p", i) for i in range(4)]),
                        (1, [("s", i) for i in range(0, 4)]),
                        (1, [("s", i) for i in range(4, 8)]),
                        (1, [("s", i) for i in range(8, 12)])):
        blks = []
        for kind, i in specs:
            blks.append(((lambda kind=kind, i=i: xsrc(xp_d[i] if kind == "p" else xs_d[i])),
                         i if kind == "p" else 4 + i))
        phase1(0, [(cond, blks)])
    zero_invalid((0, 1, 10, 11))

    if stop == 'phase1':
        return finish()
    def proj(wslot, wc0, tiles, bks, tile_outer=False):
        if tile_outer:
            for (h0, n, f0), bk in zip(tiles, bks):
                for kc in range(8):
                    P.add("pe", lambda e, kc=kc, h0=h0, n=n, bk=bk: e.matmul(
                        PS[bk][:, 0:n], lhsT=WR[:, wslot, kc, wc0:wc0 + 128], rhs=HT[:, kc, h0:h0 + n],
                        start=(kc == 0), stop=(kc == 7)),
                        reads=[("wr", wslot)] + [("ht", b_) for b_ in range(h0 // 128, (h0 + n - 1) // 128 + 1)],
                        writes=[kb(bk)])
            return
        for _ in proj_slabs(wslot, wc0, tiles, bks):
            pass

    def proj_slabs(wslot, wc0, tiles, bks):
        for kc in range(8):
            yield_after = True
            for (h0, n, f0), bk in zip(tiles, bks):
                P.add("pe", lambda e, kc=kc, h0=h0, n=n, bk=bk: e.matmul(
                    PS[bk][:, 0:n], lhsT=WR[:, wslot, kc, wc0:wc0 + 128], rhs=HT[:, kc, h0:h0 + n],
                    start=(kc == 0), stop=(kc == 7)),
                    reads=[("wr", wslot)] + [("ht", b_) for b_ in range(h0 // 128, (h0 + n - 1) // 128 + 1)],
                    writes=[kb(bk)])
            yield kc

    def fkeys(name, c, f0, n):
        return [(name, c, b_) for b_ in range(f0 // 128, (f0 + n - 1) // 128 + 1)]

    s_kv = load_w(win_e_d, 6 * 512)
    for hb in range(16):
        bk = banks(1)[0]
        for kc in range(8):
            P.add("pe", lambda e, kc=kc, hb=hb, bk=bk: e.matmul(
                PS[bk][:, 0:256], lhsT=HT[:, kc, hb * 128:(hb + 1) * 128], rhs=WR[:, s_kv, kc, 0:256],
                start=(kc == 0), stop=(kc == 7)),
                reads=[("wr", s_kv), ("ht", hb)], writes=[kb(bk)])
        for h in range(2):
            P.add("act", lambda e, h=h, hb=hb, bk=bk: e.activation(
                out=JUNK[:, 0:64], in_=PS[bk][:, h * 64:(h + 1) * 64], func=AF.Square, accum_out=RK[:, hb, h:h + 1]),
                reads=[kb(bk)], writes=[("junk",), ("rk", hb)])
        P.add("act", lambda e, hb=hb: e.activation(out=RK[:, hb, :], in_=RK[:, hb, :], func=AF.Sqrt,
                                                    scale=1.0 / 64, bias=SM[:, 127:128]),
              reads=[("rk", hb), ("sm",)], writes=[("rk", hb)])
        P.add("dve", lambda e, hb=hb: e.reciprocal(out=RK[:, hb, :], in_=RK[:, hb, :]),
              reads=[("rk", hb)], writes=[("rk", hb)])
        P.add("dve", lambda e, hb=hb: e.tensor_scalar(out=RKS[:, hb, :], in0=RK[:, hb, :], scalar1=0.125,
                                                       scalar2=None, op0=ALU.mult),
              reads=[("rk", hb)], writes=[("rks", hb)])
        for half in (0, 128):
            P.add("dve", lambda e, half=half, hb=hb, bk=bk: e.tensor_copy(
                out=VV[:, hb, :, half:half + 64], in_=PS[bk][:, 128:256].rearrange("p (k d) -> p k d", k=2)),
                reads=[kb(bk)], writes=[("vv", hb)])
        if hb < 4:
            for h in range(2):
                P.add("dve", lambda e, h=h, hb=hb, bk=bk: e.scalar_tensor_tensor(
                    out=KO[:, hb, h * 64:(h + 1) * 64], in0=PS[bk][:, h * 64:(h + 1) * 64],
                    scalar=RK[:, hb, h:h + 1], in1=GKR[:, :], op0=ALU.mult, op1=ALU.mult),
                    reads=[kb(bk), ("rk", hb), ("gkr",)], writes=[("fa", 0)])
            P.add("dve", lambda e, hb=hb, bk=bk: e.tensor_copy(out=VO[:, hb, :], in_=PS[bk][:, 128:256]),
                  reads=[kb(bk)], writes=[("fa", 1)])
            P.add("sp", lambda e, hb=hb: e.dma_start(out=nk_d[hb], in_=KO[:, hb, :]), reads=[("fa", 0)], dma=True)
            P.add("sp", lambda e, hb=hb: e.dma_start(out=nv_d[hb], in_=VO[:, hb, :]), reads=[("fa", 1)], dma=True)
    ktiles = [(0, 512), (512, 512), (1024, 512), (1536, 512)]
    kk_items = [(kap, h0, n) for kap in range(2) for (h0, n) in ktiles]
    kk_banks = [banks(1)[0] for _ in kk_items]

    def kk_proj(i):
        kap, h0, n = kk_items[i]
        bk = kk_banks[i]
        for kc in range(8):
            P.add("pe", lambda e, kc=kc, h0=h0, n=n, bk=bk, kap=kap: e.matmul(
                PS[bk][:, 0:n], lhsT=WR[:, s_kv, kc, 256 + kap * 128:256 + (kap + 1) * 128],
                rhs=HT[:, kc, h0:h0 + n], start=(kc == 0), stop=(kc == 7)),
                reads=[("wr", s_kv)] + [("ht", h0 // 128 + i2) for i2 in range(4)], writes=[kb(bk)])

    kk_proj(0)
    for i, (kap, h0, n) in enumerate(kk_items):
        bk = kk_banks[i]
        kkw = [("kk", kap, h0 // 128 + i2) for i2 in range(4)]
        if h0 == 0:
            P.add("act", lambda e, bk=bk, kap=kap: e.activation(
                out=KK[:, kap, 0:512], in_=PS[bk][:, 0:512], func=AF.Identity, scale=SM[:, C_GK:C_GK + 1]),
                reads=[kb(bk), ("sm",)], writes=kkw)
            if i + 1 < len(kk_items):
                kk_proj(i + 1)
        else:
            t0 = h0 - 512
            qg, kqg = QGS_K[i % 2]
            P.add("act", lambda e, bk=bk, qg=qg: e.activation(
                out=qg[:, :], in_=PS[bk][:, 0:512], func=AF.Identity, scale=SM[:, C_GK:C_GK + 1]),
                reads=[kb(bk), ("sm",)], writes=[kqg])
            if i + 1 < len(kk_items):
                kk_proj(i + 1)
            P.add("pe", lambda e, bk=bk, qg=qg: e.matmul(PS[bk][:, 0:512], lhsT=CB[:, CB_PM:CB_PM + 128], rhs=qg[:, :],
                                                  start=True, stop=True),
                  reads=[("cb",), kqg], writes=[kb(bk)])
            P.add("dve", lambda e, bk=bk, t0=t0: e.tensor_tensor(out=FC[:, 0:512], in0=PS[bk][:, 0:512],
                                                                 in1=TABS[:, t0:t0 + 512], op=ALU.mult),
                  reads=[kb(bk), ("tab",)], writes=[("fc", 0)])
            P.add("dve", lambda e, t0=t0, qg=qg: e.tensor_tensor(out=FC[:, 512:1024], in0=qg[:, :],
                                                          in1=TABC[:, t0:t0 + 512], op=ALU.mult),
                  reads=[kqg, ("tab",)], writes=[("fc", 1)])
            P.add("dve", lambda e, kap=kap, h0=h0: e.tensor_tensor(out=KK[:, kap, h0:h0 + 512], in0=FC[:, 0:512],
                                                                  in1=FC[:, 512:1024], op=ALU.add),
                  reads=[("fc", 0), ("fc", 1)], writes=kkw)

    if stop == 'kv':
        return finish()
    for c0_, c1_ in ((512, 512 + 128 - HALO), (1536 + 128 + HALO, NFULL)):
        P.add("dve", lambda e, c0_=c0_, c1_=c1_: e.memset(AT[:, :, c0_:c1_], 0.0),
              writes=[("at", c_, b_) for c_ in range(8) for b_ in range(c0_ // 128, (c1_ - 1) // 128 + 1)])
    CX = FA[:, 0:CXW]
    ACC = FB[:, 0:CXW]
    P.add("dve", lambda e: e.memset(FA[:, :], 0.0), writes=[("fa", i) for i in range(4)])

    def cseg(buf, ti):
        if ti == 0:
            return buf[:, 1:515].rearrange("p (s c) -> p s c", c=257)[:, :, 0:256]
        f0 = TILES[ti][2] - 512
        n = TILES[ti][1]
        return buf[:, 515 + f0:515 + f0 + n]

    def pseg(bk, ti):
        if ti == 0:
            return PS[bk][:, 0:512].rearrange("p (s c) -> p s c", c=256)
        return PS[bk][:, 0:TILES[ti][1]]

    fa_all = [("fa", i) for i in range(4)]
    fb_all = [("fb", i) for i in range(4)]
    for j in range(4):
        run_deferred(1)
        s = load_w(win_e_d, j * 512)
        bks = banks(4)
        proj(s, 128, TILES, bks)
        for ti in range(4):
            P.add("act", lambda e, ti=ti, bk=bks[ti]: e.activation(out=cseg(CX, ti), in_=pseg(bk, ti), func=AF.Copy),
                  reads=[kb(bks[ti])], writes=fa_all)
        bks = banks(4)
        proj(s, 256, TILES, bks)
        for ti in range(4):
            P.add("dve", lambda e, ti=ti, bk=bks[ti]: e.tensor_tensor(out=cseg(CX, ti), in0=cseg(CX, ti),
                                                                      in1=pseg(bk, ti), op=ALU.mult),
                  reads=[kb(bks[ti])] + fa_all, writes=fa_all)
        P.add("dve", lambda e, j=j: e.tensor_scalar(out=ACC[:, 1:CXW - 1], in0=CX[:, 0:CXW - 2],
                                                    scalar1=SM[:, C_CW + j:C_CW + j + 1], scalar2=SM[:, C_CB + j:C_CB + j + 1],
                                                    op0=ALU.mult, op1=ALU.add),
              reads=fa_all + [("sm",)], writes=fb_all)
        P.add("dve", lambda e, j=j: e.scalar_tensor_tensor(out=ACC[:, 1:CXW - 1], in0=CX[:, 1:CXW - 1],
                                                           scalar=SM[:, C_CW + 4 + j:C_CW + 5 + j], in1=ACC[:, 1:CXW - 1],
                                                           op0=ALU.mult, op1=ALU.add),
              reads=fa_all + fb_all + [("sm",)], writes=fb_all)
        P.add("dve", lambda e, j=j: e.scalar_tensor_tensor(out=ACC[:, 1:CXW - 1], in0=CX[:, 2:CXW],
                                                           scalar=SM[:, C_CW + 8 + j:C_CW + 9 + j], in1=ACC[:, 1:CXW - 1],
                                                           op0=ALU.mult, op1=ALU.add),
              reads=fa_all + fb_all + [("sm",)], writes=fb_all)
        bks = banks(4)
        proj(s, 0, TILES, bks)
        for ti in range(4):
            P.add("dve", lambda e, ti=ti, bk=bks[ti]: e.tensor_tensor(out=cseg(ACC, ti), in0=cseg(ACC, ti),
                                                                      in1=pseg(bk, ti), op=ALU.mult),
                  reads=[kb(bks[ti])] + fb_all, writes=fb_all)
        bks = banks(4)
        proj(s, 384, TILES, bks)
        for ti, (h0, n, f0) in enumerate(TILES):
            P.add("act", lambda e, ti=ti, bk=bks[ti], n=n, f0=f0: e.activation(
                out=SGT[:, 0, f0:f0 + n], in_=PS[bk][:, 0:n], func=AF.Silu),
                reads=[kb(bks[ti])], writes=fkeys("sgt", 0, f0, n))
        for ti, (h0, n, f0) in enumerate(TILES):
            if ti == 0:
                o = AT[:, j, 0:512].rearrange("p (s c) -> p s c", c=256)
                i1 = SGT[:, 0, 0:512].rearrange("p (s c) -> p s c", c=256)
            else:
                o = AT[:, j, f0:f0 + n]
                i1 = SGT[:, 0, f0:f0 + n]
            P.add("dve", lambda e, ti=ti, o=o, i1=i1: e.tensor_tensor(out=o, in0=cseg(ACC, ti), in1=i1, op=ALU.mult),
                  reads=fb_all + fkeys("sgt", 0, f0, n), writes=fkeys("at", j, f0, n))

    if stop == 'conv':
        return finish()
    s_q = load_w(win_e_d, 4 * 512)
    s_gb = load_w(win_e_d, 5 * 512)
    assert s_q != s_gb
    s_free = 3 - s_q - s_gb
    RSETS = [(FC[:, 0:512], FC[:, 512:1024], FC[:, 1024:1536], ("fc", 0), ("fc", 1), ("fc", 2)),
             (FA[:, 0:512], FA[:, 512:1024], FA[:, 1024:1536], ("fa", 0), ("fa", 1), ("fa", 2))]
    QGS = [(QG, ("qg",)), (QG2, ("qg2",))]
    SQS = [(SQ, ("sq",)), (SQ2, ("sq2",))]
    tctr_q = [0]

    def q_post(c, bks):
        for _ in q_post_gen(c, bks):
            pass

    def q_post_gen(c, bks):
        ctx = []
        for ti, (h0, n, f0) in enumerate(TILES):
            k2 = tctr_q[0] % 2
            tctr_q[0] += 1
            ctx.append((ti, h0, n, f0, bks[ti], RSETS[k2], QGS[k2], SQS[k2]))

        def stage_a(t):
            ti, h0, n, f0, bk, (R1, R2, R3, kr1, kr2, kr3), (qg, kqg), (sq, ksq) = ctx[t]
            P.add("act", lambda e: e.activation(out=sq[:, 0:n], in_=PS[bk][:, 0:n], func=AF.Square),
                  reads=[kb(bk)], writes=[ksq])
            P.add("act", lambda e: e.activation(out=qg[:, 0:n], in_=PS[bk][:, 0:n], func=AF.Identity,
                                                scale=SM[:, C_GQ:C_GQ + 1]),
                  reads=[kb(bk), ("sm",)], writes=[kqg])

        def stage_ssq(t):
            ti, h0, n, f0, bk, (R1, R2, R3, kr1, kr2, kr3), (qg, kqg), (sq, ksq) = ctx[t]
            P.add("pe", lambda e: e.matmul(PS[bk][:, 0:n], lhsT=CB[:, CB_BO:CB_BO + 128], rhs=sq[:, 0:n],
                                           start=True, stop=True),
                  reads=[("cb",), ksq], writes=[kb(bk)])

        def stage_rstd(t):
            ti, h0, n, f0, bk, (R1, R2, R3, kr1, kr2, kr3), (qg, kqg), (sq, ksq) = ctx[t]
            P.add("act", lambda e: e.activation(out=R1[:, 0:n], in_=PS[bk][:, 0:n], func=AF.Ln,
                                                scale=1.0 / 64, bias=SM[:, 127:128]),
                  reads=[kb(bk), ("sm",)], writes=[kr1])
            P.add("act", lambda e: e.activation(out=R1[:, 0:n], in_=R1[:, 0:n], func=AF.Exp, scale=-0.5),
                  reads=[kr1], writes=[kr1])

        def stage_pm(t):
            ti, h0, n, f0, bk, (R1, R2, R3, kr1, kr2, kr3), (qg, kqg), (sq, ksq) = ctx[t]
            if ti != 0:
                P.add("pe", lambda e: e.matmul(PS[bk][:, 0:n], lhsT=CB[:, CB_PM:CB_PM + 128], rhs=qg[:, 0:n],
                                               start=True, stop=True),
                      reads=[("cb",), kqg], writes=[kb(bk)])

        def stage_dve(t):
            ti, h0, n, f0, bk, (R1, R2, R3, kr1, kr2, kr3), (qg, kqg), (sq, ksq) = ctx[t]
            qk = fkeys("qt", c, f0, n)
            if ti == 0:
                P.add("dve", lambda e: e.tensor_tensor(out=QT[:, c, f0:f0 + n], in0=qg[:, 0:n], in1=R1[:, 0:n],
                                                       op=ALU.mult),
                      reads=[kqg, kr1], writes=qk)
                return
            t0 = h0 - 512
            P.add("dve", lambda e: e.tensor_tensor(out=R2[:, 0:n], in0=PS[bk][:, 0:n], in1=TABS[:, t0:t0 + n],
                                                   op=ALU.mult),
                  reads=[kb(bk), ("tab",)], writes=[kr2])
            P.add("dve", lambda e: e.tensor_tensor(out=R3[:, 0:n], in0=qg[:, 0:n], in1=TABC[:, t0:t0 + n], op=ALU.mult),
                  reads=[kqg, ("tab",)], writes=[kr3])
            P.add("dve", lambda e: e.tensor_tensor(out=R3[:, 0:n], in0=R3[:, 0:n], in1=R2[:, 0:n], op=ALU.add),
                  reads=[kr2, kr3], writes=[kr3])
            P.add("dve", lambda e: e.tensor_tensor(out=QT[:, c, f0:f0 + n], in0=R3[:, 0:n], in1=R1[:, 0:n],
                                                   op=ALU.mult),
                  reads=[kr3, kr1], writes=qk)

        stage_a(0)
        yield "a"
        stage_ssq(0)
        yield "b"
        for t in range(4):
            if t + 1 < 4:
                stage_a(t + 1)
                yield "a"
            stage_rstd(t)
            stage_pm(t)
            yield "c"
            if t + 1 < 4:
                stage_ssq(t + 1)
                yield "b"
            stage_dve(t)

    def gb_post(c, bks):
        for ti, (h0, n, f0) in enumerate(TILES):
            P.add("act", lambda e, bk=bks[ti], n=n, f0=f0, c=c: e.activation(
                out=AT[:, 4 + c, f0:f0 + n], in_=PS[bk][:, 0:n], func=AF.Silu),
                reads=[kb(bks[ti])], writes=fkeys("at", 4 + c, f0, n))

    SETA, SETB = [0, 1, 2, 3], [4, 5, 6, 7]
    for c in range(4):
        bset = SETA if c % 2 == 0 else SETB
        proj(s_gb, c * 128, TILES, bset)
        gb_post(c, bset)
    proj(s_q, 0, TILES, SETA)
    for c in range(4):
        cur_set = SETA if c % 2 == 0 else SETB
        oth_set = SETB if c % 2 == 0 else SETA
        slabs = proj_slabs(s_q, (c + 1) * 128, TILES, oth_set) if c < 3 else iter(())
        for _ in q_post_gen(c, cur_set):
            next(slabs, None)
        for _ in slabs:
            pass
        run_deferred(1, slot=s_free, bank=cur_set[0])
    bank_ctr[0] = 0

    if stop == 'q':
        return finish()
    P.add("dve", lambda e: e.memset(GBT[:, :], 1.0), writes=[("gbt",)])

    def gate_tile(L, cond):
        dst, dkeys = (GT, [("gt",)]) if cond == 0 else (FC[:, 0:1024], [("fc", 0), ("fc", 1)])
        DG = FB[:, 0:1024]
        for kc in range(8):
            P.add("dve", lambda e, kc=kc: e.tensor_scalar(out=DG[:, kc * 128:(kc + 1) * 128], in0=IDF[:, :],
                                                          scalar1=gatev(L, cond, kc), scalar2=None, op0=ALU.mult),
                  reads=[("idf",), ("ma", L, 2, 0), ("ma", L, 2, 1)], writes=[("fb", kc // 4)])
        bks = banks(2)
        for h in range(2):
            P.add("pe", lambda e, h=h, bk=bks[h]: e.matmul(PS[bk][:, :], lhsT=GBT[:, :], rhs=DG[:, h * 512:(h + 1) * 512],
                                                           start=True, stop=True),
                  reads=[("gbt",), ("fb", h)], writes=[kb(bks[h])])
            P.add("act", lambda e, h=h, bk=bks[h], dst=dst: e.activation(out=dst[:, h * 512:(h + 1) * 512], in_=PS[bk][:, :],
                                                                func=AF.Copy),
                  reads=[kb(bks[h])], writes=dkeys)
        return dst, dkeys

    gts_l0 = {0: gate_tile(0, 0), 1: gate_tile(0, 1)}

    DBG = os.environ.get("ATT_DBG", "")
    items = []
    for qb in range(14):
        if qb < 4:
            s0 = (qb // 2) * 2
            klist = [(s0, s0, None, None), (s0 + 1, s0 + 1, None, None)]
            fq = qb * 128
        else:
            wb = qb - 3
            fq = 512 + (wb - 1) * 128

            def vx(w):
                return {0: 0, 1: 1, 10: 2, 11: 3}.get(w)
            klist = [(4 + wb - 1, 4 + wb - 1, CB_NML, vx(wb - 1)), (4 + wb, 4 + wb, None, vx(wb)),
                     (4 + wb + 1, 4 + wb + 1, CB_NMR, vx(wb + 1)), ("c", 16, None, None), ("c", 17, None, None)]
        for kap in range(2):
            gi = qb * 2 + kap
            for ki, kl in enumerate(klist):
                q0, nq = (112, 16) if qb == 4 else ((0, 16) if qb == 13 else (0, 128))
                items.append(dict(qb=qb, kap=kap, fq=fq, ki=ki, nk=len(klist), kl=kl, ob=gi % 2, n=len(items),
                                  q0=q0, nq=nq))

    def emit_S(it):
        n, kap, fq, q0, nq = it["n"], it["kap"], it["fq"], it["q0"], it["nq"]
        W = 2 * nq
        kblk, vblk, mcol, vxi = it["kl"]
        sp_ = 2 + 2 * (n % 3)
        sbanks = (sp_, sp_ + 1)
        masked = mcol is not None
        if masked:
            mrhs = CB[:, mcol:mcol + 256] if nq == 128 else \
                CB[:, mcol:mcol + 256].rearrange("p (j q) -> p j q", j=2)[:, :, q0:q0 + nq]
            for hb_ in range(2):
                P.add("pe", lambda e, sbk=sbanks[hb_], mrhs=mrhs: e.matmul(
                    PS[sbk][:, 0:W], lhsT=CB[:, CB_ID:CB_ID + 128], rhs=mrhs, start=True, stop=False),
                    reads=[("cb",)], writes=[kb(sbanks[hb_])])
        for hb_ in range(2):
            base = hb_ * 64
            sbk = sbanks[hb_]
            if kblk == "c":
                kt = KC[base:base + 64, kap, (vblk - 16) * 128:(vblk - 15) * 128]
                kr = [("kc",)]
            else:
                kt = KK[base:base + 64, kap, kblk * 128:(kblk + 1) * 128]
                kr = [("kk", kap, kblk)]
            sout = PS[sbk][:, 0:256].rearrange("p (j q) -> p j q", j=2) if nq == 128 else PS[sbk][:, 0:W]
            P.add("pe", lambda e, sout=sout, kt=kt, base=base, kap=kap, masked=masked: e.matmul(
                sout, lhsT=kt, rhs=QT[base:base + 64, 2 * kap:2 * kap + 2, fq + q0:fq + q0 + nq],
                start=(not masked), stop=True),
                reads=kr + [("qt", 2 * kap, fq // 128), ("qt", 2 * kap + 1, fq // 128)], writes=[kb(sbk)])
        es_ = n % 4
        for hb_ in range(2):
            sbk = sbanks[hb_]
            if kblk == "c":
                P.add("act", lambda e, sbk=sbk, es_=es_, hb_=hb_: e.activation(
                    out=ET[:, es_, hb_ * 256:hb_ * 256 + W], in_=PS[sbk][:, 0:W], func=AF.Exp, scale=0.125),
                    reads=[kb(sbk)], writes=[("et", es_, hb_)])
            else:
                P.add("act", lambda e, sbk=sbk, es_=es_, kblk=kblk, kap=kap, hb_=hb_: e.activation(
                    out=ET[:, es_, hb_ * 256:hb_ * 256 + W], in_=PS[sbk][:, 0:W], func=AF.Exp,
                    scale=RKS[:, kblk, kap:kap + 1]),
                    reads=[kb(sbk), ("rks", kblk)], writes=[("et", es_, hb_)])

    def emit_PV(it):
        n, kap, fq, ob, ki, q0, nq = it["n"], it["kap"], it["fq"], it["ob"], it["ki"], it["q0"], it["nq"]
        W = 2 * nq
        kblk, vblk, mcol, vxi = it["kl"]
        es_ = n % 4
        for half in range(2):
            for which in (0, 1):
                if which == 0:
                    lt = VV[:, vblk, kap, half * 64:half * 64 + 128]
                    lr = [("vv", vblk)]
                elif vxi is not None:
                    lt = VALX[:, vxi, half * 64:half * 64 + 128]
                    lr = [("valx", vxi)]
                else:
                    lt = CB[:, CB_O3 + half * 64:CB_O3 + half * 64 + 128]
                    lr = [("cb",)]
                stt = (ki == 0) and (half == 0) and (which == 0)
                stp = (ki == it["nk"] - 1) and (half == 1) and (which == 1)
                P.add("pe", lambda e, ob=ob, lt=lt, es_=es_, half=half, which=which, stt=stt, stp=stp: e.matmul(
                    PS[ob][:, which * 256:which * 256 + W], lhsT=lt, rhs=ET[:, es_, half * 256:half * 256 + W],
                    start=stt, stop=stp),
                    reads=lr + [("et", es_, half)], writes=[kb(ob)])
        if ki != it["nk"] - 1:
            return
        if it["nk"] == 2:
            for jj in range(2):
                c = 2 * kap + jj
                P.add("act", lambda e, ob=ob, jj=jj, c=c: e.activation(
                    out=RD[:, jj * nq:(jj + 1) * nq], in_=PS[ob][:, 256 + jj * nq:256 + (jj + 1) * nq],
                    func=AF.Ln, bias=EST[:, c:c + 1]),
                    reads=[kb(ob), ("est",)], writes=[("rd",)])
            P.add("act", lambda e: e.activation(out=RD[:, 0:W], in_=RD[:, 0:W], func=AF.Exp, scale=-1.0),
                  reads=[("rd",)], writes=[("rd",)])
        else:
            for jj in range(2):
                c = 2 * kap + jj
                P.add("dve", lambda e, ob=ob, jj=jj, c=c: e.tensor_scalar(
                    out=RD[:, jj * nq:(jj + 1) * nq], in0=PS[ob][:, 256 + jj * nq:256 + (jj + 1) * nq],
                    scalar1=EST[:, c:c + 1], scalar2=None, op0=ALU.add),
                    reads=[kb(ob), ("est",)], writes=[("rd",)])
            P.add("dve", lambda e: e.reciprocal(out=RD[:, 0:W], in_=RD[:, 0:W]), reads=[("rd",)], writes=[("rd",)])
        atk = [("at", 4 + 2 * kap, fq // 128), ("at", 5 + 2 * kap, fq // 128)]
        rd3 = RD[:, 0:W].rearrange("p (j q) -> p j q", j=2)
        P.add("dve", lambda e, kap=kap: e.tensor_tensor(
            out=rd3, in0=rd3, in1=AT[:, 4 + 2 * kap:6 + 2 * kap, fq + q0:fq + q0 + nq], op=ALU.mult),
            reads=[("rd",)] + atk, writes=[("rd",)])
        P.add("dve", lambda e, kap=kap, ob=ob: e.tensor_tensor(
            out=AT[:, 4 + 2 * kap:6 + 2 * kap, fq + q0:fq + q0 + nq],
            in0=PS[ob][:, 0:W].rearrange("p (j q) -> p j q", j=2), in1=rd3, op=ALU.mult),
            reads=[kb(ob), ("rd",)], writes=atk)

    LOOK = 2
    for it in items[:LOOK]:
        emit_S(it)
    for n, it in enumerate(items):
        emit_PV(it)
        if n + LOOK < len(items):
            emit_S(items[n + LOOK])

    if stop == 'attn':
        return finish()
    TMPS = [TMP, TMP2]

    def xring_load(i, xd):
        sl = i % 2
        P.add("sp", lambda e: e.dma_start(out=FA[:, sl * 1024:(sl + 1) * 1024], in_=xd),
              writes=[("fa", 2 * sl), ("fa", 2 * sl + 1)], dma=True)

    def outproj_prep(L, wd, blocks, slots=None, gts=None):
        if slots is None:
            s0 = load_w(wd, 0)
            s1 = load_w(wd, 512)
        else:
            s0, s1 = slots
        if gts is None:
            gts = {0: gate_tile(L, 0), 1: gate_tile(L, 1)}
        if L == 0:
            for i in range(2):
                xring_load(i, blocks[i][3])
        return (L, s0, s1, blocks, gts)

    def outproj_run(state, after_block=None):
        L, s0, s1, blocks, gts = state
        tctr = [0]
        for bi, (cond, yb, f0, xd, od) in enumerate(blocks):
            GTc, gtk = gts[cond]
            for nt, s in ((0, s0), (1, s1)):
                bk = banks(1)[0]
                for kc in range(8):
                    P.add("pe", lambda e, kc=kc, bk=bk, s=s, f0=f0: e.matmul(
                        PS[bk][:, :], lhsT=AT[:, kc, f0:f0 + 128], rhs=WR[:, s, kc, :], start=(kc == 0), stop=(kc == 7)),
                        reads=[("wr", s), ("at", kc, f0 // 128)], writes=[kb(bk)])
                if L == 0:
                    P.add("dve", lambda e, bk=bk, nt=nt, yb=yb, GTc=GTc: e.tensor_tensor(
                        out=YS(yb)[:, nt * 512:(nt + 1) * 512], in0=PS[bk][:, :], in1=GTc[:, nt * 512:(nt + 1) * 512],
                        op=ALU.mult),
                        reads=[kb(bk)] + gtk, writes=[("ys", yb)])
                else:
                    tm = TMPS[tctr[0] % 2]
                    tk = ("tmp", tctr[0] % 2)
                    tctr[0] += 1
                    P.add("dve", lambda e, bk=bk, nt=nt, tm=tm, GTc=GTc: e.tensor_tensor(
                        out=tm[:, :], in0=PS[bk][:, :], in1=GTc[:, nt * 512:(nt + 1) * 512], op=ALU.mult),
                        reads=[kb(bk)] + gtk, writes=[tk])
                    P.add("dve", lambda e, yb=yb, nt=nt, tm=tm: e.tensor_tensor(
                        out=YS(yb)[:, nt * 512:(nt + 1) * 512], in0=YS(yb)[:, nt * 512:(nt + 1) * 512], in1=tm[:, :],
                        op=ALU.add),
                        reads=[tk, ("ys", yb)], writes=[("ys", yb)])
            if L == 0:
                sl = bi % 2
                P.add("pool", lambda e, yb=yb, sl=sl: e.tensor_tensor(
                    out=YS(yb), in0=YS(yb), in1=FA[:, sl * 1024:(sl + 1) * 1024], op=ALU.add),
                    reads=[("ys", yb), ("fa", 2 * sl), ("fa", 2 * sl + 1)], writes=[("ys", yb)])
                if bi + 2 < len(blocks):
                    xring_load(bi + 2, blocks[bi + 2][3])
            if od is not None:
                P.add("sp", lambda e, yb=yb, od=od: e.dma_start(out=od, in_=YS(yb)), reads=[("ys", yb)], dma=True)
            if after_block is not None:
                after_block(bi)

    run_deferred(8)
    blocks0 = [(0, pb, pb * 128, xp_d[pb], None) for pb in range(4)]
    blocks0 += [(1, 4 + (wb - 1), 512 + (wb - 1) * 128, xs_d[wb], None) for wb in range(1, 11)]
    st0 = outproj_prep(0, wout_e_d, blocks0, gts=gts_l0)
    wslot1 = {0: load_w(win_o_d, 0)}
    assert wslot1[0] not in (st0[1], st0[2])
    P.barrier()

    def ysrc(b):
        return YS(b), [("ys", b)]
    g1 = {3: (0, [((lambda pb=pb: ysrc(pb)), pb) for pb in range(4)])}
    for lo, hi in ((1, 5), (5, 9), (9, 11)):
        g1[4 + hi - 2] = (1, [((lambda wb=wb: ysrc(4 + wb - 1)), 4 + wb) for wb in range(lo, hi)])

    pend = [None]

    def after_l0_block(bi):
        if bi in g1:
            if pend[0] is not None:
                phase1_b(pend[0])
            pend[0] = phase1_a(1, g1[bi])
    outproj_run(st0, after_block=after_l0_block)
    phase1_b(pend[0])
    zero_invalid((1, 10))

    if stop == 'out0':
        return finish()
    U = FA[:, 0:UW]
    PA = FB[:, 0:UW]
    PB = FC[:, 0:UW]
    P.add("dve", lambda e: e.memset(FA[:, :], 0.0), writes=fa_all)
    fc_all = [("fc", i) for i in range(4)]

    def useg(buf, ti):
        if ti == 0:
            return buf[:, 16:592].rearrange("p (s c) -> p s c", c=288)[:, :, 0:256]
        f0 = TILES[ti][2] - 512
        n = TILES[ti][1]
        return buf[:, 592 + f0:592 + f0 + n]


    def l1_pool(g, cc):
        s = wslot1[g]
        bks = banks(4)
        proj(s, cc * 128, TILES, bks, tile_outer=(g == 0 and cc == 0))
        return bks

    def l1_pool_post(g, cc, bks):
        w = (2, 4, 8, 16)[g]
        if True:
            for ti in range(4):
                P.add("act", lambda e, ti=ti, bk=bks[ti]: e.activation(out=useg(U, ti), in_=pseg(bk, ti), func=AF.Copy),
                      reads=[kb(bks[ti])], writes=fa_all)
            lo, hi = 1, UW - 1
            P.add("dve", lambda e, lo=lo, hi=hi: e.tensor_tensor(out=PA[:, lo:hi], in0=U[:, lo:hi], in1=U[:, lo - 1:hi - 1],
                                                                 op=ALU.add),
                  reads=fa_all, writes=fb_all)
            cur, curk, oth, othk = PA, fb_all, PB, fc_all
            for sh in (1, 2, 4)[:g]:
                lo, hi = lo + sh, hi - sh
                P.add("dve", lambda e, lo=lo, hi=hi, sh=sh, cur=cur, oth=oth: e.tensor_tensor(
                    out=oth[:, lo:hi], in0=cur[:, lo + sh:hi + sh], in1=cur[:, lo - sh:hi - sh], op=ALU.add),
                    reads=curk, writes=othk)
                cur, curk, oth, othk = oth, othk, cur, curk
            for ti, (h0, n, f0) in enumerate(TILES):
                if ti == 0:
                    o = YT[:, cc, 0:512].rearrange("p (s c) -> p s c", c=256)
                else:
                    o = YT[:, cc, f0:f0 + n]
                P.add("dve", lambda e, ti=ti, o=o, cur=cur, w=w: e.scalar_tensor_tensor(
                    out=o, in0=useg(cur, ti), scalar=1.0 / w, in1=useg(U, ti), op0=ALU.mult, op1=ALU.subtract),
                    reads=curk + fa_all, writes=fkeys("yt", cc, f0, n))
            for sq in range(2):
                for ed in range(2):
                    c0 = 16 + sq * 288 + (0 if ed == 0 else 248)
                    fcol = sq * 256 + (0 if ed == 0 else 248)
                    tcol = C_PCE + g * 16 + ed * 8
                    P.add("pool", lambda e, c0=c0, tcol=tcol, cur=cur: e.tensor_tensor(
                        out=TMP3[:, 0:8], in0=cur[:, c0:c0 + 8], in1=SM[:, tcol:tcol + 8], op=ALU.mult),
                        reads=curk + [("sm",)], writes=[("tmp3",)])
                    P.add("pool", lambda e, c0=c0, fcol=fcol, cc=cc: e.tensor_tensor(
                        out=YT[:, cc, fcol:fcol + 8], in0=TMP3[:, 0:8], in1=U[:, c0:c0 + 8], op=ALU.subtract),
                        reads=[("tmp3",)] + fa_all, writes=fkeys("yt", cc, (fcol // 128) * 128, 128))
            for ed in range(2):
                sc = 128 if ed == 0 else 1144
                c0 = 592 + sc
                fcol = 512 + sc
                tcol = C_SCE + g * 16 + ed * 8
                P.add("pool", lambda e, c0=c0, tcol=tcol, cur=cur: e.tensor_tensor(
                    out=TMP3[:, 8:16], in0=cur[:, c0:c0 + 8], in1=SM[:, tcol:tcol + 8], op=ALU.mult),
                    reads=curk + [("sm",)], writes=[("tmp3b",)])
                P.add("pool", lambda e, c0=c0, fcol=fcol, cc=cc: e.tensor_tensor(
                    out=YT[:, cc, fcol:fcol + 8], in0=TMP3[:, 8:16], in1=U[:, c0:c0 + 8], op=ALU.subtract),
                    reads=[("tmp3b",)] + fa_all, writes=fkeys("yt", cc, (fcol // 128) * 128, 128))

    def l1_projg(g, cc):
        s = wslot1[g]
        bks = banks(3)
        proj(s, 256 + cc * 128, TILES_G, bks)
        for ti, (h0, n, f0) in enumerate(TILES_G):
            P.add("act", lambda e, bk=bks[ti], n=n, f0=f0, cc=cc: e.activation(
                out=SGT[:, cc, f0:f0 + n], in_=PS[bk][:, 0:n], func=AF.Silu),
                reads=[kb(bks[ti])], writes=fkeys("sgt", cc, f0, n))

    def l1_poolw(g, avoid=()):
        free = [b for b in range(8) if b not in avoid]
        pc = [0]
        for dd in range(2):
            for ti, (h0, n, f0) in enumerate(TILES_G):
                bk = free[pc[0] % len(free)]
                pc[0] += 1
                for cc in range(2):
                    P.add("pe", lambda e, bk=bk, cc=cc, dd=dd, n=n, f0=f0, g=g: e.matmul(
                        PS[bk][:, 0:n], lhsT=PW[:, g, cc, dd * 128:(dd + 1) * 128], rhs=YT[:, cc, f0:f0 + n],
                        start=(cc == 0), stop=(cc == 1)),
                        reads=[("pw",)] + fkeys("yt", cc, f0, n), writes=[kb(bk)])
                ch = 2 * g + dd
                P.add("dve", lambda e, bk=bk, n=n, f0=f0, dd=dd, ch=ch: e.scalar_tensor_tensor(
                    out=AT[:, ch, f0:f0 + n], in0=PS[bk][:, 0:n], scalar=SM[:, C_PSC + ch:C_PSC + ch + 1],
                    in1=SGT[:, dd, f0:f0 + n], op0=ALU.mult, op1=ALU.mult),
                    reads=[kb(bk), ("sm",)] + fkeys("sgt", dd, f0, n), writes=fkeys("at", ch, f0, n))

    for g in range(1, 3):
        wslot1[g] = load_w(win_o_d, g * 512)
    for g in range(4):
        if g == 1:
            wslot1[3] = load_w(win_o_d, 3 * 512)
        elif g == 2:
            wslot1["o0"] = load_w(wout_o_d, 0)
        elif g == 3:
            wslot1["o1"] = load_w(wout_o_d, 512)
        bks0 = l1_pool(g, 0)
        if g > 0:
            l1_poolw(g - 1, avoid=bks0)
        l1_pool_post(g, 0, bks0)
        bks1 = l1_pool(g, 1)
        l1_pool_post(g, 1, bks1)
        l1_projg(g, 0)
        l1_projg(g, 1)
    l1_poolw(3)

    blocks1 = [(0, pb, pb * 128, None, yp_d[pb]) for pb in range(4)]
    blocks1 += [(1, 4 + (wb - 1), 512 + (wb - 1) * 128, None, ysm_d[wb - 2]) for wb in range(2, 10)]
    outproj_run(outproj_prep(1, wout_o_d, blocks1, slots=(wslot1["o0"], wslot1["o1"])))

    return finish()


_NC_CACHE = {}


def _host_tables(side):
    a = side * 1024
    j = np.arange(1536)
    t = a - 256 + j
    tv = np.clip(t, 0, 2047)
    row = (tv // 64).astype(np.float64)
    col = (tv % 64).astype(np.float64)
    inv = 10000.0 ** (-np.arange(16, dtype=np.float64) / 16)
    p = np.arange(128)
    d = p % 64
    f = d % 16
    pos = np.where((d < 32)[:, None], row[None, :], col[None, :])
    ang = pos * inv[f][:, None]
    tabc = np.cos(ang).astype(np.float32)
    sign = np.where((d % 32) < 16, -1.0, 1.0)
    tabs = (np.sin(ang) * sign[:, None]).astype(np.float32)
    return tabc, tabs


def _consts():
    cb = np.zeros((128, NCB), np.float32)
    cb[:, CB_ID:CB_ID + 128] = np.eye(128, dtype=np.float32)
    m = np.arange(128)
    partner = np.where((m % 32) < 16, m + 16, m - 16)
    pm = np.zeros((128, 128), np.float32)
    pm[partner, m] = 1.0
    cb[:, CB_PM:CB_PM + 128] = pm
    cb[:, CB_BO:CB_BO + 128] = (m[:, None] // 64 == m[None, :] // 64).astype(np.float32)
    kk = m[:, None]
    qq = m[None, :]
    for rep in range(2):
        cb[:, CB_NML + rep * 128:CB_NML + (rep + 1) * 128] = np.where(qq <= kk, 0.0, NEGM)
        cb[:, CB_NMR + rep * 128:CB_NMR + (rep + 1) * 128] = np.where(kk <= qq, 0.0, NEGM)
    cb[:, CB_O3:CB_O3 + 64] = 1.0
    cb[:, CB_O3 + 128:CB_O3 + 192] = 1.0
    return cb


def make_in_maps(x_prompt, x_sample, cache_k, cache_v, c, c_ctx,
                 ada_w_e, ada_b_e, norm_g_e, w_in_e, conv_w, conv_b, q_norm_g, k_norm_g, sink, w_out_e,
                 ada_w_o, ada_b_o, norm_g_o, w_in_o, pool_w, pool_scale, w_out_o):
    f = lambda a: np.ascontiguousarray(np.asarray(a, dtype=np.float32))
    x_prompt, x_sample, cache_k, cache_v, c, c_ctx = map(f, (x_prompt, x_sample, cache_k, cache_v, c, c_ctx))
    ada_w_e, ada_b_e, norm_g_e, w_in_e, conv_w, conv_b = map(f, (ada_w_e, ada_b_e, norm_g_e, w_in_e, conv_w, conv_b))
    q_norm_g, k_norm_g, sink, w_out_e = map(f, (q_norm_g, k_norm_g, sink, w_out_e))
    ada_w_o, ada_b_o, norm_g_o, w_in_o, pool_w, pool_scale, w_out_o = map(
        f, (ada_w_o, ada_b_o, norm_g_o, w_in_o, pool_w, pool_scale, w_out_o))

    W = w_in_e[0]
    bg, cg, xs_, ga, q_, k_, v_, gb = (W[:, 0:512], W[:, 512:1024], W[:, 1024:1536], W[:, 1536:2048],
                                       W[:, 2048:2560], W[:, 2560:2688], W[:, 2688:2816], W[:, 2816:3328])
    groups = []
    for j in range(4):
        sl = slice(j * 128, (j + 1) * 128)
        groups += [bg[:, sl], cg[:, sl], xs_[:, sl], ga[:, sl]]
    groups += [q_, gb, k_, v_, k_[:, 0:64], k_[:, 0:64], k_[:, 64:128], k_[:, 64:128]]
    win_e = np.ascontiguousarray(np.concatenate(groups, axis=1))
    Wo = w_in_o[0]
    go = []
    for g in range(4):
        go += [Wo[:, g * 256:(g + 1) * 256], Wo[:, 1024 + g * 256:1024 + (g + 1) * 256]]
    win_o = np.ascontiguousarray(np.concatenate(go, axis=1))
    cb = _consts()
    idf = np.eye(128, dtype=np.float32)
    p = np.arange(128)

    def fm(v, n):
        return v.reshape(n, 128).T

    in_maps = []
    for r in range(8):
        b, side = r // 2, r % 2
        a = side * 1024
        xp = x_prompt[2 * r:2 * r + 2].reshape(4, 128, D)
        xs = np.zeros((12 * 128, D), np.float32)
        t = a - 256 + np.arange(1536)
        ok = (t >= 0) & (t < 2048)
        xs[ok] = x_sample[b][t[ok]]
        vflag = ok.reshape(12, 128)[:, 0].astype(np.float32)
        smalls = np.zeros((128, NSMALL), np.float32)
        cond = np.stack([fm(c_ctx, 8), fm(c[b], 8)], axis=2)
        smalls[:, C_COND:C_COND + 16] = cond.reshape(128, 16)
        smalls[:, C_ABE:C_ABE + 24] = fm(ada_b_e[0], 24)
        smalls[:, C_ABO:C_ABO + 24] = fm(ada_b_o[0], 24)
        smalls[:, C_NGE:C_NGE + 8] = fm(norm_g_e[0], 8)
        smalls[:, C_NGO:C_NGO + 8] = fm(norm_g_o[0], 8)
        for k in range(3):
            smalls[:, C_CW + k * 4:C_CW + k * 4 + 4] = fm(conv_w[0, k], 4)
        smalls[:, C_CB:C_CB + 4] = fm(conv_b[0], 4)
        smalls[:, C_GQ] = q_norm_g[0][p % 64]
        smalls[:, C_GK] = k_norm_g[0][p % 64]
        for cc in range(4):
            smalls[:, C_SINK + cc] = np.where(p >= 64, sink[0, 2 * cc + 1], sink[0, 2 * cc])
        smalls[:, C_PSC:C_PSC + 8] = fm(pool_scale[0], 8)
        smalls[:, C_VF:C_VF + 12] = vflag[None, :]
        smalls[:, 127] = EPS
        for g, w in enumerate((2, 4, 8, 16)):
            L = 2048
            tt = np.arange(L)
            lo = np.clip(tt - w // 2, 0, L)
            hi = np.clip(tt + w // 2, 0, L)
            rc = (1.0 / (hi - lo)).astype(np.float32)
            smalls[:, C_PCE + g * 16:C_PCE + g * 16 + 8] = rc[None, 0:8]
            smalls[:, C_PCE + g * 16 + 8:C_PCE + g * 16 + 16] = rc[None, L - 8:L]
            smalls[:, C_SCE + g * 16:C_SCE + g * 16 + 8] = rc[None, 0:8] if side == 0 else 1.0 / w
            smalls[:, C_SCE + g * 16 + 8:C_SCE + g * 16 + 16] = rc[None, L - 8:L] if side == 1 else 1.0 / w
        tabc, tabs = _host_tables(side)
        ck = cache_k[b, 0]
        kc = np.ascontiguousarray(np.transpose(ck, (2, 1, 0)))
        kc = np.concatenate([kc, kc], axis=0)
        cv = np.ascontiguousarray(cache_v[b, 0].reshape(2, 128, 128).transpose(1, 0, 2))
        in_maps.append(dict(
            xp=np.ascontiguousarray(xp), xs=xs.reshape(12, 128, D), smalls=smalls,
            gkrow=np.ascontiguousarray(np.broadcast_to(k_norm_g[0][None, :], (128, 64))),
            tabc=tabc, tabs=tabs, cb=cb, idf=idf, kc=np.ascontiguousarray(kc), cv=cv,
            wada_e=ada_w_e[0], wada_o=ada_w_o[0], win_e=win_e, wout_e=w_out_e[0], win_o=win_o,
            wout_o=w_out_o[0], pw=pool_w[0]))

    return in_maps


def kernel(**inputs):
    in_maps = make_in_maps(**inputs)
    if "nc" not in _NC_CACHE:
        _NC_CACHE["nc"] = build_program()
    nc = _NC_CACHE["nc"]
    res = run_bass_kernel_spmd(nc, in_maps, core_ids=list(range(8)))
    yp = np.zeros((16, 256, D), np.float32)
    ys = np.zeros((4, 2048, D), np.float32)
    nk = np.zeros((16, 1, 256, 2, 64), np.float32)
    nv = np.zeros((16, 1, 256, 2, 64), np.float32)
    for r in range(8):
        o = res.results[r]
        b, side = r // 2, r % 2
        yp[2 * r:2 * r + 2] = np.asarray(o["yp"]).reshape(2, 256, D)
        ys[b, side * 1024:(side + 1) * 1024] = np.asarray(o["ysm"]).reshape(1024, D)
        nk[2 * r:2 * r + 2, 0] = np.asarray(o["nk"]).reshape(2, 256, 2, 64)
        nv[2 * r:2 * r + 2, 0] = np.asarray(o["nv"]).reshape(2, 256, 2, 64)
    return yp, ys, nk, nv
```
